# Optimizing a Trainium2 kernel written in Bass

```python
import jax, jax.numpy as jnp
from jax import lax
import numpy as np

D_MODEL = 2048
BATCH = 2
SEQ = 4096
DEPTH = 1

GRID_W = 64
N_MEM = 256

NA_HEADS = 8
NA_HEAD_DIM = 128
NA_MAX_ROWS = 8
NA_COLS = 16

RET_HEADS = 8
RET_QK_DIM = 128
RET_V_DIM = 256
RET_CHUNK = 128
ROPE_BASE = 10000.0

XA_HEADS = 4
XA_HEAD_DIM = 256

D_FF = 4 * D_MODEL
N_BRANCH = 3
EPS = 1e-6

NA_W = NA_HEADS * NA_HEAD_DIM
RET_QK_W = RET_HEADS * RET_QK_DIM
RET_V_W = RET_HEADS * RET_V_DIM
XA_W = XA_HEADS * XA_HEAD_DIM
IN_SPLITS = (NA_W, NA_W, NA_W, RET_QK_W, RET_QK_W, RET_V_W, RET_V_W, XA_W, D_MODEL, D_MODEL, D_MODEL)
D_IN = sum(IN_SPLITS)

kernel_name = 'hybrid_na_retention_memxattn_encoder'


def rmsnorm(x, g):
    xf = x.astype(jnp.float32)
    y = xf * lax.rsqrt(jnp.mean(jnp.square(xf), axis=-1, keepdims=True) + EPS)
    return (y * g.astype(jnp.float32)).astype(x.dtype)


def split_heads(t, n_heads):
    b, s, w = t.shape
    return t.reshape(b, s, n_heads, w // n_heads).transpose(0, 2, 1, 3)


def merge_heads(t):
    b, h, s, d = t.shape
    return t.transpose(0, 2, 1, 3).reshape(b, s, h * d)


def rope(t):
    s, d = t.shape[2], t.shape[3]
    half = d // 2
    inv = jnp.power(jnp.float32(ROPE_BASE), -jnp.arange(half, dtype=jnp.float32) / half)
    ang = jnp.arange(s, dtype=jnp.float32)[:, None] * inv[None, :]
    cos = jnp.cos(ang).astype(t.dtype)
    sin = jnp.sin(ang).astype(t.dtype)
    t1, t2 = t[..., :half], t[..., half:]
    return jnp.concatenate([t1 * cos - t2 * sin, t1 * sin + t2 * cos], axis=-1)


def neighbourhood_attention(q, k, v, rpb):
    b, h, s, dh = q.shape
    rows = s // GRID_W
    kr = min(NA_MAX_ROWS, rows)
    kc = NA_COLS
    r = jnp.arange(rows)
    row_start = jnp.clip(r - kr // 2, 0, rows - kr)
    row_idx = row_start[:, None] + jnp.arange(kr)[None, :]
    c = jnp.arange(GRID_W)
    col_start = jnp.clip(c - kc // 2, 0, GRID_W - kc)
    col_ok = (c[None, :] >= col_start[:, None]) & (c[None, :] < col_start[:, None] + kc)
    qg = q.reshape(b, h, rows, GRID_W, dh)
    kg = k.reshape(b, h, rows, GRID_W, dh)[:, :, row_idx]
    vg = v.reshape(b, h, rows, GRID_W, dh)[:, :, row_idx]
    sc = jnp.einsum('bhrqd,bhrkwd->bhrqkw', qg, kg).astype(jnp.float32)
    dr = row_idx - r[:, None] + (NA_MAX_ROWS - 1)
    dc = jnp.clip(c[None, :] - c[:, None], -(kc - 1), kc - 1) + (kc - 1)
    bias = rpb[:, dr[:, None, :, None], dc[None, :, None, :]].astype(jnp.float32)
    sc = sc + bias[None]
    sc = jnp.where(col_ok[:, None, :], sc, -jnp.inf)
    p = jax.nn.softmax(sc.reshape(b, h, rows, GRID_W, kr * GRID_W), axis=-1)
    p = p.reshape(sc.shape).astype(v.dtype)
    o = jnp.einsum('bhrqkw,bhrkwd->bhrqd', p, vg)
    return o.reshape(b, h, s, dh)


def retention_one_direction(q, k, v, log_g, strict):
    b, h, s, dk = q.shape
    dv = v.shape[-1]
    c = RET_CHUNK
    n = s // c
    dt = q.dtype
    qc = q.reshape(b, h, n, c, dk)
    kc = k.reshape(b, h, n, c, dk)
    vc = v.reshape(b, h, n, c, dv)
    i = jnp.arange(c, dtype=jnp.float32)
    diff = i[:, None] - i[None, :]
    mask = (diff > 0) if strict else (diff >= 0)
    d_intra = jnp.where(mask[None], jnp.exp(log_g[:, None, None] * jnp.maximum(diff, 0.0)[None]), 0.0).astype(dt)
    sc = jnp.einsum('bhnid,bhnjd->bhnij', qc, kc) * d_intra[:, None]
    o_intra = jnp.einsum('bhnij,bhnje->bhnie', sc, vc)
    k_decay = jnp.exp(log_g[:, None] * (c - 1 - i)[None]).astype(dt)
    q_decay = jnp.exp(log_g[:, None] * (i + 1)[None]).astype(dt)
    kv = jnp.einsum('bhncd,bhnce->nbhde', kc * k_decay[:, None, :, None], vc)
    chunk_decay = jnp.exp(log_g * c).astype(dt)[:, None, None]

    def step(state, kv_n):
        return chunk_decay * state + kv_n, state

    _, states = lax.scan(step, jnp.zeros((b, h, dk, dv), kv.dtype), kv)
    o_inter = jnp.einsum('bhncd,nbhde->bhnce', qc * q_decay[:, None, :, None], states)
    return (o_intra + o_inter).reshape(b, h, s, dv)


def head_group_norm(o, g):
    of = o.astype(jnp.float32)
    mu = jnp.mean(of, axis=-1, keepdims=True)
    var = jnp.mean(jnp.square(of - mu), axis=-1, keepdims=True)
    y = (of - mu) * lax.rsqrt(var + EPS)
    return (merge_heads(y) * g.astype(jnp.float32)).astype(o.dtype)


def setup_inputs(seed: int = 0) -> dict:
    key = jax.random.key(seed)
    ks = jax.random.split(key, 24)
    f32 = jnp.float32

    def nrm(k, shape, scale):
        return jax.random.normal(k, shape, f32) * scale

    def gain(k, shape):
        return 1.0 + 0.02 * jax.random.normal(k, shape, f32)

    e = 5.0 + np.arange(RET_HEADS, dtype=np.float32)
    base_logit = jnp.asarray(np.log(np.power(2.0, e) - 1.0).astype(np.float32))
    return {
        'x': nrm(ks[0], (BATCH, SEQ, D_MODEL), 1.0),
        'mem': nrm(ks[1], (BATCH, N_MEM, D_MODEL), 1.0),
        'norm_mix_g': gain(ks[2], (DEPTH, D_MODEL)),
        'w_in': nrm(ks[3], (DEPTH, D_MODEL, D_IN), D_MODEL ** -0.5),
        'na_q_norm_g': gain(ks[4], (DEPTH, NA_HEAD_DIM)),
        'na_k_norm_g': gain(ks[5], (DEPTH, NA_HEAD_DIM)),
        'na_rpb': nrm(ks[6], (DEPTH, NA_HEADS, 2 * NA_MAX_ROWS - 1, 2 * NA_COLS - 1), 0.02),
        'ret_decay_logit_fwd': base_logit[None] + nrm(ks[7], (DEPTH, RET_HEADS), 0.01),
        'ret_decay_logit_bwd': base_logit[None] + nrm(ks[8], (DEPTH, RET_HEADS), 0.01),
        'ret_gn_g': gain(ks[9], (DEPTH, RET_V_W)),
        'mem_norm_g': gain(ks[10], (DEPTH, D_MODEL)),
        'w_mem_kv': nrm(ks[11], (DEPTH, D_MODEL, 2 * XA_W), D_MODEL ** -0.5),
        'xa_q_norm_g': gain(ks[12], (DEPTH, XA_HEAD_DIM)),
        'xa_k_norm_g': gain(ks[13], (DEPTH, XA_HEAD_DIM)),
        'w_br_na': nrm(ks[14], (DEPTH, NA_W, D_MODEL), NA_W ** -0.5),
        'w_br_ret': nrm(ks[15], (DEPTH, RET_V_W, D_MODEL), RET_V_W ** -0.5),
        'w_br_mem': nrm(ks[16], (DEPTH, XA_W, D_MODEL), XA_W ** -0.5),
        'w_out': nrm(ks[17], (DEPTH, D_MODEL, D_MODEL), D_MODEL ** -0.5),
        'norm_ffn_g': gain(ks[18], (DEPTH, D_MODEL)),
        'w_ff1': nrm(ks[19], (DEPTH, D_MODEL, D_FF), D_MODEL ** -0.5),
        'w_ff2': nrm(ks[20], (DEPTH, D_FF, D_MODEL), D_FF ** -0.5),
    }


def reference(x, mem, norm_mix_g, w_in, na_q_norm_g, na_k_norm_g, na_rpb,
              ret_decay_logit_fwd, ret_decay_logit_bwd, ret_gn_g, mem_norm_g, w_mem_kv,
              xa_q_norm_g, xa_k_norm_g, w_br_na, w_br_ret, w_br_mem, w_out,
              norm_ffn_g, w_ff1, w_ff2):
    split_at = [int(v) for v in np.cumsum(IN_SPLITS)[:-1]]
    for l in range(DEPTH):
        h = rmsnorm(x, norm_mix_g[l])
        proj = h @ w_in[l]
        (na_q, na_k, na_v, rq, rk, rv, rg, xq, g_na, g_ret, g_mem) = jnp.split(proj, split_at, axis=-1)

        qa = rmsnorm(split_heads(na_q, NA_HEADS), na_q_norm_g[l]) * (NA_HEAD_DIM ** -0.5)
        ka = rmsnorm(split_heads(na_k, NA_HEADS), na_k_norm_g[l])
        va = split_heads(na_v, NA_HEADS)
        o_na = merge_heads(neighbourhood_attention(qa, ka, va, na_rpb[l]))

        qr = rope(split_heads(rq, RET_HEADS))
        kr = rope(split_heads(rk, RET_HEADS)) * (RET_QK_DIM ** -0.5)
        vr = split_heads(rv, RET_HEADS)
        lg_f = jax.nn.log_sigmoid(ret_decay_logit_fwd[l].astype(jnp.float32))
        lg_b = jax.nn.log_sigmoid(ret_decay_logit_bwd[l].astype(jnp.float32))
        o_f = retention_one_direction(qr, kr, vr, lg_f, False)
        o_b = jnp.flip(retention_one_direction(jnp.flip(qr, 2), jnp.flip(kr, 2), jnp.flip(vr, 2), lg_b, True), 2)
        o_ret = head_group_norm(o_f + o_b, ret_gn_g[l]) * jax.nn.silu(rg)

        mkv = rmsnorm(mem, mem_norm_g[l]) @ w_mem_kv[l]
        mk, mv = jnp.split(mkv, 2, axis=-1)
        qx = rmsnorm(split_heads(xq, XA_HEADS), xa_q_norm_g[l]) * (XA_HEAD_DIM ** -0.5)
        kx = rmsnorm(split_heads(mk, XA_HEADS), xa_k_norm_g[l])
        vx = split_heads(mv, XA_HEADS)
        px = jax.nn.softmax(jnp.einsum('bhsd,bhmd->bhsm', qx, kx).astype(jnp.float32), axis=-1).astype(vx.dtype)
        o_mem = merge_heads(jnp.einsum('bhsm,bhmd->bhsd', px, vx))

        merged = (jax.nn.sigmoid(g_na) * (o_na @ w_br_na[l])
                  + jax.nn.sigmoid(g_ret) * (o_ret @ w_br_ret[l])
                  + jax.nn.sigmoid(g_mem) * (o_mem @ w_br_mem[l]))
        x = x + merged @ w_out[l]

        h2 = rmsnorm(x, norm_ffn_g[l])
        x = x + jnp.square(jax.nn.relu(h2 @ w_ff1[l])) @ w_ff2[l]
    return x
```

```python
import numpy as np
import concourse.bass as bass
import concourse.mybir as mybir
from concourse.bass_utils import run_bass_kernel_spmd

F32 = mybir.dt.float32
BF16 = mybir.dt.bfloat16
AF = mybir.ActivationFunctionType
ALU = mybir.AluOpType

ENGS = ("pe", "act", "dve", "pool", "sp")

P = 128
D = 2048
TOK = 1024
NT = TOK // P
KC = D // P
EPS = 1e-6
DFF = 8192
C_NAQ, C_NAK, C_NAV, C_RQ, C_RK, C_RV, C_RG, C_XQ, C_GNA, C_GRET, C_GMEM = (
    0, 1024, 2048, 3072, 4096, 5120, 7168, 9216, 10240, 12288, 14336)
NSLOT = 4
BIG = 1.0e6
NEG = -30000.0


class Buf:
    __slots__ = ("name", "lw", "rd", "inh")

    def __init__(self, name, inh=None):
        self.name = name
        self.lw = None
        self.rd = {}
        self.inh = dict(inh) if inh else None


class Sched:
    def __init__(self, nc):
        self.nc = nc
        self.ops = []
        self.eng_ops = {e: [] for e in ENGS}
        self.dma_sems = {}
        self.final_waits = []

    def op(self, eng, fn, r=(), w=(), dma_key=None):
        deps = {}

        def add(i, kind):
            if i is None:
                return
            old = deps.get(i)
            if old is None or (old == "war" and kind != "war"):
                deps[i] = kind

        for b in r:
            if b.inh:
                for i in b.inh:
                    add(i, "raw")
            add(b.lw, "raw")
        for b in w:
            if b.inh:
                for i in b.inh:
                    add(i, "raw")
                b.inh = None
            add(b.lw, "waw")
            for i in b.rd.values():
                add(i, "war")
        oid = len(self.ops)
        rec = dict(id=oid, eng=eng, fn=fn, deps=deps, dma=dma_key, pos=len(self.eng_ops[eng]),
                   sig=False, ticket=None)
        self.ops.append(rec)
        self.eng_ops[eng].append(rec)
        for b in r:
            if dma_key is not None:
                b.rd[("d", oid)] = oid
            else:
                b.rd[eng] = oid
        for b in w:
            b.lw = oid
            b.rd = {}
        return oid

    def barrier_deps(self):
        d = {}
        for e in ENGS:
            for rec in reversed(self.eng_ops[e]):
                if rec["dma"] is None:
                    d[rec["id"]] = "raw"
                    break
        latest = {}
        for rec in self.ops:
            if rec["dma"] is not None:
                latest[rec["dma"]] = rec["id"]
        for i in latest.values():
            d[i] = "raw"
        return d

    def _needed(self, rec, dep):
        kind = rec["deps"][dep["id"]]
        if dep["dma"] is not None or rec["dma"] is not None:
            return True
        if dep["eng"] != rec["eng"]:
            return True
        if rec["eng"] == "pe" or kind == "war":
            return False
        return (rec["pos"] - dep["pos"]) <= 3

    def emit(self, block):
        nc = self.nc
        ops = self.ops
        for rec in ops:
            for d in rec["deps"]:
                dep = ops[d]
                if dep["dma"] is None and self._needed(rec, dep):
                    dep["sig"] = True
        eng_sem = {e: nc.alloc_semaphore("sem_" + e) for e in ENGS}
        cnt = {e: 0 for e in ENGS}
        for e in ENGS:
            for rec in self.eng_ops[e]:
                if rec["dma"] is not None:
                    ent = self.dma_sems.get(rec["dma"])
                    if ent is None:
                        ent = [nc.alloc_semaphore("dsem_%d" % len(self.dma_sems)), 0]
                        self.dma_sems[rec["dma"]] = ent
                    ent[1] += 16
                    rec["sem"] = ent[0]
                    rec["ticket"] = ent[1]
                    if isinstance(rec["dma"], str) and rec["dma"].startswith("G:"):
                        ent.append(rec)
                elif rec["sig"]:
                    cnt[e] += 1
                    rec["sem"] = eng_sem[e]
                    rec["ticket"] = cnt[e]
        self.counts = cnt
        for ent in self.dma_sems.values():
            for rec in ent[2:]:
                rec["ticket"] = ent[1]

        def make(e):
            def body(h):
                waited = {}
                for rec in self.eng_ops[e]:
                    for d in sorted(rec["deps"]):
                        dep = ops[d]
                        if not self._needed(rec, dep):
                            continue
                        key = id(dep["sem"])
                        if waited.get(key, 0) >= dep["ticket"]:
                            continue
                        h.wait_ge(dep["sem"], dep["ticket"])
                        waited[key] = dep["ticket"]
                    ins = rec["fn"](h)
                    if rec["dma"] is not None:
                        ins.then_inc(rec["sem"], 16)
                    elif rec["sig"]:
                        ins.then_inc(rec["sem"], 1)
                if e == "sp":
                    for oid in self.final_waits:
                        rr = ops[oid]
                        h.wait_ge(rr["sem"], rr["ticket"])
            return body

        block.tensor(make("pe"))
        block.scalar(make("act"))
        block.vector(make("dve"))
        block.gpsimd(make("pool"))
        block.sync(make("sp"))


_DTSZ = {F32: 4, BF16: 2}


class K:
    def __init__(self, dbg=()):
        self.nc = nc = bass.Bass("TRN2", target_bir_lowering=False)
        self.s = Sched(nc)
        self.dbg = set(dbg)
        self.din = {}
        self.uid = 0
        self.ap_ptr = (int(nc.sbuf_base) + 63) // 64 * 64
        self.ap_top = int(nc.sbuf_top)
        self.inh = None
        self.nalloc = 0

    def dram_in(self, name, shape, dt=F32):
        t = self.nc.dram_tensor(name, list(shape), dt, kind="ExternalInput")
        self.din[name] = t
        return t.ap()

    def sb(self, name, shape, dt, at=None):
        n = 1
        for v in shape[1:]:
            n *= v
        nbytes = n * _DTSZ[dt]
        if at is None:
            off = self.ap_ptr
            self.ap_ptr += (nbytes + 63) // 64 * 64
            assert self.ap_ptr <= self.ap_top, ("SBUF overflow", name, self.ap_ptr, self.ap_top)
        else:
            off = at
        self.nalloc += 1
        t = self.nc.alloc_sbuf_tensor_at("%s_%d" % (name, self.nalloc), list(shape), dt, offset=off)
        return t, Buf(name, self.inh)

    def mark(self):
        return self.ap_ptr

    def release(self, m):
        self.ap_ptr = m
        self.inh = self.s.barrier_deps()

    def dma(self, q, out, in_, r=(), w=(), key=None):
        if key is None:
            self.uid += 1
            key = ("u", self.uid)
        return self.s.op(q, lambda h, o=out, i=in_: h.dma_start(out=o, in_=i), r=r, w=w, dma_key=key)

    def act(self, out, in_, func, r=(), w=(), scale=None, bias=None, accum_out=None):
        kw = {}
        if scale is not None:
            kw["scale"] = scale
        if bias is not None:
            kw["bias"] = bias
        if accum_out is not None:
            kw["accum_out"] = accum_out
        return self.s.op("act", lambda h: h.activation(out=out, in_=in_, func=func, **kw), r=r, w=w)

    def tt(self, out, in0, in1, op, r=(), w=(), eng="dve"):
        return self.s.op(eng, lambda h: h.tensor_tensor(out=out, in0=in0, in1=in1, op=op), r=r, w=w)

    def ts(self, out, in0, s1, op0, s2=None, op1=None, r=(), w=(), eng="dve"):
        if op1 is None:
            return self.s.op(eng, lambda h: h.tensor_scalar(out=out, in0=in0, scalar1=s1, scalar2=None,
                                                            op0=op0), r=r, w=w)
        return self.s.op(eng, lambda h: h.tensor_scalar(out=out, in0=in0, scalar1=s1, scalar2=s2,
                                                        op0=op0, op1=op1), r=r, w=w)

    def stt(self, out, in0, scalar, in1, op0, op1, r=(), w=()):
        return self.s.op("dve", lambda h: h.scalar_tensor_tensor(out=out, in0=in0, scalar=scalar, in1=in1,
                                                                 op0=op0, op1=op1), r=r, w=w)

    def copy(self, out, in_, r=(), w=(), eng="dve"):
        if eng == "act":
            return self.act(out, in_, AF.Copy, r=r, w=w)
        return self.s.op(eng, lambda h: h.tensor_copy(out=out, in_=in_), r=r, w=w)

    def recip(self, out, in_, r=(), w=()):
        return self.s.op("dve", lambda h: h.reciprocal(out=out, in_=in_), r=r, w=w)

    def memset(self, ap, val, w=(), eng="dve"):
        return self.s.op(eng, lambda h: h.memset(ap, val), w=w)

    def mm(self, out, pairs, r=(), w=(), start=True, stop=True):
        def fn(h):
            n = len(pairs)
            ins = None
            for i, (l, rh) in enumerate(pairs):
                ins = h.matmul(out, lhsT=l, rhs=rh, start=(start and i == 0), stop=(stop and i == n - 1))
            return ins
        return self.s.op("pe", fn, r=r, w=w)

    def mmg(self, groups, r=(), w=()):
        def fn(h):
            ins = None
            for out, pairs in groups:
                n = len(pairs)
                for i, (l, rh) in enumerate(pairs):
                    ins = h.matmul(out, lhsT=l, rhs=rh, start=(i == 0), stop=(i == n - 1))
            return ins
        return self.s.op("pe", fn, r=r, w=w)

    def tr(self, outs_ins, ident, r=(), w=()):
        def fn(h):
            ins = None
            for o, i in outs_ins:
                ins = h.transpose(out=o, in_=i, identity=ident)
            return ins
        return self.s.op("pe", fn, r=r, w=w)

    def dump(self, name, ap, buf, shape, dt):
        if name not in self.dbg:
            return
        o = self.nc.dram_tensor("dbg_" + name, list(shape), dt, kind="ExternalOutput").ap()
        oid = self.dma("sp", o, ap, r=[buf])
        self.s.final_waits.append(oid)


PHASES = "ADCBEFG"


def build(dbg=(), upto="G"):
    k = K(dbg)
    nc = k.nc
    s = k.s
    last = PHASES.index(upto)

    def on(ph):
        return PHASES.index(ph) <= last

    x_own = k.dram_in("x_own", [TOK, D])
    x_oth = k.dram_in("x_oth", [3 * TOK, D])
    x_halo = k.dram_in("x_halo", [TOK, D])
    mem_in = k.dram_in("mem_b", [256, D])
    g_mix = k.dram_in("g_mix", [1, D])
    g_ffn = k.dram_in("g_ffn", [1, D])
    g_mem = k.dram_in("g_mem", [1, D])
    gn_g = k.dram_in("gn_g", [1, D])
    ident_in = k.dram_in("ident_in", [P, P])
    rope_own_in = k.dram_in("rope_own", [P, 8, 2, 64])
    rope_oth_in = k.dram_in("rope_oth", [P, 24, 2, 64])
    e_oth_in = k.dram_in("e_oth", [P, 2, 24])
    e_own_in = k.dram_in("e_own", [P, 4])
    dmask_in = k.dram_in("dmask", [P, 4, P])
    dlog_in = k.dram_in("dlog", [1, 16])
    nag_in = k.dram_in("nag", [P, 2])
    xag_in = k.dram_in("xag", [P, 4])
    nab_in = k.dram_in("na_bias", [P, 8, 4096])
    w_in = k.dram_in("w_in", [D, 16384])
    w_mkv = k.dram_in("w_mem_kv", [D, 2048])
    w_bna = k.dram_in("w_br_na", [1024, D])
    w_bret = k.dram_in("w_br_ret", [2048, D])
    w_bmem = k.dram_in("w_br_mem", [1024, D])
    w_out = k.dram_in("w_out", [D, D])
    w_ff1 = k.dram_in("w_ff1", [D, DFF])
    w_ff2 = k.dram_in("w_ff2", [DFF, D])
    y_out = nc.dram_tensor("y_own", [TOK, D], F32, kind="ExternalOutput").ap()

    psb = []
    for i in range(8):
        t = nc.alloc_psum_tensor("ps%d" % i, [P, 512], F32)
        psb.append((t, Buf("ps%d" % i)))
    k.psi = 0

    k.ps_ring = list(range(8))

    def ps():
        t, b = psb[k.ps_ring[k.psi % len(k.ps_ring)]]
        k.psi += 1
        return t, b

    ident_f, ident_fb = k.sb("ident_f", [P, P], F32)
    ident, identb = k.sb("ident", [P, P], BF16)
    ones, onesb = k.sb("ones", [P, P], BF16)
    k.dma("sp", ident_f[:], ident_in, w=[ident_fb], key="G:c")
    k.copy(ident[:], ident_f[:], r=[ident_fb], w=[identb])
    k.memset(ones[:], 1.0, w=[onesb], eng="dve")
    wring = [k.sb("wr%d" % i, [P, 8, 512], BF16) for i in range(NSLOT)]
    k.wi = 0
    hT, hTb = k.sb("hT", [P, KC, TOK], BF16)

    def wload(w_dram, r0, segs):
        slot = k.wi % NSLOT
        t, b = wring[slot]
        k.wi += 1
        o = 0
        for (c0, ncols) in segs:
            src = w_dram[r0:r0 + 1024, c0:c0 + ncols].rearrange("(k p) c -> p k c", p=P)
            k.dma("pool", t[:, :, o:o + ncols], src, w=[b], key=("w", slot))
            o += ncols
        return t, b

    def wpair(w_dram, segs):
        return wload(w_dram, 0, segs), wload(w_dram, 1024, segs)

    class Stage:
        pass

    def make_stage(nxs=2):
        st = Stage()
        st.xs = [k.sb("xs%d" % i, [P, D], F32) for i in range(nxs)]
        st.xn = [k.sb("xn%d" % i, [P, D], BF16) for i in range(2)]
        st.st = [k.sb("st%d" % i, [P, 4], F32) for i in range(2)]
        st.gbc, st.gbcb = k.sb("gbc", [P, D], F32)
        return st

    def load_gain(st, g_dram):
        k.dma("sp", st.gbc[:], g_dram.partition_broadcast(P), w=[st.gbcb])

    def rms1(st, i, src_dram=None, src_sb=None):
        nt, nb = st.xn[i % 2]
        stt_, stb = st.st[i % 2]
        if src_sb is None:
            xt, xb = st.xs[i % len(st.xs)]
            k.dma("sp", xt[:], src_dram, w=[xb], key=("xs", i % len(st.xs)))
            xa = xt[:]
        else:
            xa, xb = src_sb
        k.act(nt[:], xa, AF.Square, r=[xb], w=[nb, stb], accum_out=stt_[:, 0:1], scale=float(D) ** -0.5)
        k.act(stt_[:, 1:2], stt_[:, 0:1], AF.Sqrt, r=[stb], w=[stb], bias=EPS)
        k.recip(stt_[:, 2:3], stt_[:, 1:2], r=[stb], w=[stb])
        k.stt(nt[:], xa, stt_[:, 2:3], st.gbc[:], ALU.mult, ALU.mult, r=[xb, stb, st.gbcb], w=[nb])

    def rms2(st, i, dstT, dstTb, col0):
        nt, nb = st.xn[i % 2]
        for half in range(2):
            pt, pb = ps()
            pv = pt[:].bitcast(BF16)
            k.tr([(pv[:, c * P:(c + 1) * P], nt[:, (half * 8 + c) * P:(half * 8 + c + 1) * P]) for c in range(8)],
                 ident[:], r=[nb, identb], w=[pb])
            k.copy(dstT[:, half * 8:(half + 1) * 8, col0:col0 + P],
                   pv.rearrange("p (c t) -> p c t", c=8), r=[pb], w=[dstTb],
                   eng=("act" if half == 0 else "dve"))

    def rmsnorm_tile(st, i, dstT, dstTb, col0, src_dram=None, src_sb=None):
        rms1(st, i, src_dram=src_dram, src_sb=src_sb)
        rms2(st, i, dstT, dstTb, col0)

    def rmsnorm_seq(st, n, dstT, dstTb_of, col_of, src_of):
        rms1(st, 0, **src_of(0))
        for i in range(n):
            if i + 1 < n:
                rms1(st, i + 1, **src_of(i + 1))
            rms2(st, i, dstT, dstTb_of(i), col_of(i))

    def gemm_tok(pout, pb, actT, actb, tok0, wA, wB, c0, ncols, kA=8, kB=8):
        (ta, ba), (tb_, bb) = wA, wB
        k.mm(pout, [(actT[:, kc, tok0:tok0 + P], ta[:, kc, c0:c0 + ncols]) for kc in range(kA)],
             r=[actb, ba], w=[pb], start=True, stop=(kB == 0))
        if kB:
            k.mm(pout, [(actT[:, 8 + kc, tok0:tok0 + P], tb_[:, kc, c0:c0 + ncols]) for kc in range(kB)],
                 r=[actb, bb], w=[pb], start=False, stop=True)

    def gemm_feat(pout, pb, srcs, wA, wB, c0):
        (ta, ba), (tb_, bb) = wA, wB
        af, actb = srcs
        k.mm(pout, [(ta[:, kc, c0:c0 + P], af(kc)) for kc in range(8)], r=[actb, ba], w=[pb],
             start=True, stop=False)
        k.mm(pout, [(tb_[:, kc, c0:c0 + P], af(8 + kc)) for kc in range(8)], r=[actb, bb], w=[pb],
             start=False, stop=True)

    m0 = k.mark()
    st = make_stage(4)
    load_gain(st, g_mix)
    rmsnorm_seq(st, NT, hT, lambda i: hTb, lambda i: i * P, lambda i: dict(src_dram=x_own[i * P:(i + 1) * P, :]))
    k.dump("hT", hT[:], hTb, [P, KC, TOK], BF16)

    if on("D"):
        k.release(m0)
        oreg = k.mark()
        k.ap_ptr += 65536
        mD = k.mark()
        lg, lgb = k.sb("lg", [P, 16], F32)
        eown, eownb = k.sb("eown", [P, 4], F32)
        wq3, wq3b = k.sb("wq3", [P, 3, 8], F32)
        wk3, wk3b = k.sb("wk3", [P, 3, 8], F32)
        gC, gCb = k.sb("gC", [P, 16], F32)
        MT, MTb = k.sb("MT", [P, 8, P], F32)
        eo, eob = k.sb("eo", [P, 2, 24], F32)
        wfo, wfob = k.sb("wfo", [P, 2, 24, 8], F32)
        S_in, S_inb = k.sb("S_in", [P, 2, 8, 256], F32)
        rope_own, rope_ownb = k.sb("rope_own", [P, 8, 2, 64], F32)
        rout = [k.sb("rout%d" % i, [P, 4, 2, 64], F32) for i in range(2)]
        rX = [k.sb("rX%d" % i, [P, 4, 2, 64], F32) for i in range(1)]
        rY = [k.sb("rY%d" % i, [P, 4, 2, 64], F32) for i in range(1)]
        mDt = k.mark()
        dl, dlb = k.sb("dl", [P, 16], F32)
        dm, dmb = k.sb("dm", [P, 4, P], F32)
        mtmp, mtmpb = k.sb("mtmp", [P, 2, P], F32)
        k.dma("sp", dl[:], dlog_in.partition_broadcast(P), w=[dlb])
        k.dma("sp", eown[:], e_own_in, w=[eownb])
        k.dma("sp", dm[:], dmask_in, w=[dmb])
        k.dma("sp", eo[:], e_oth_in, w=[eob])
        k.dma("sp", rope_own[:], rope_own_in, w=[rope_ownb])
        sc = float(P) ** -0.5
        k.act(lg[:], dl[:], AF.Exp, r=[dlb], w=[lgb], scale=-1.0)
        k.act(lg[:], lg[:], AF.Ln, r=[lgb], w=[lgb], bias=1.0)
        k.ts(lg[:], lg[:], -1.0, ALU.mult, r=[lgb], w=[lgb])
        k.memset(wq3[:, 0, :], 1.0, w=[wq3b], eng="dve")
        k.act(wq3[:, 1, :], lg[:, 0:8], AF.Exp, r=[lgb, eownb], w=[wq3b], scale=eown[:, 0:1])
        k.act(wq3[:, 2, :], lg[:, 8:16], AF.Exp, r=[lgb, eownb], w=[wq3b], scale=eown[:, 1:2])
        k.memset(wk3[:, 0, :], sc, w=[wk3b], eng="dve")
        k.act(wk3[:, 1, :], lg[:, 0:8], AF.Exp, r=[lgb, eownb], w=[wk3b], scale=eown[:, 2:3])
        k.act(wk3[:, 2, :], lg[:, 8:16], AF.Exp, r=[lgb, eownb], w=[wk3b], scale=eown[:, 3:4])
        k.ts(wk3[:, 1:3, :], wk3[:, 1:3, :], sc, ALU.mult, r=[wk3b], w=[wk3b])
        k.act(gC[:], lg[:], AF.Exp, r=[lgb], w=[gCb], scale=float(P))
        for h in range(8):
            k.act(mtmp[:, 0, :], dm[:, 0, :], AF.Exp, r=[dmb, lgb], w=[mtmpb], scale=lg[:, h:h + 1])
            k.act(mtmp[:, 1, :], dm[:, 2, :], AF.Exp, r=[dmb, lgb], w=[mtmpb], scale=lg[:, 8 + h:9 + h])
            k.tt(mtmp[:], mtmp[:], dm[:, 1:4:2, :], ALU.mult, r=[mtmpb, dmb], w=[mtmpb])
            k.tt(MT[:, h, :], mtmp[:, 0, :], mtmp[:, 1, :], ALU.add, r=[mtmpb], w=[MTb])
        for d_ in range(2):
            k.tt(wfo[:, d_], eo[:, d_, :].unsqueeze(2).to_broadcast([P, 24, 8]),
                 lg[:, 8 * d_:8 * d_ + 8].unsqueeze(1).to_broadcast([P, 24, 8]), ALU.mult,
                 r=[eob, lgb], w=[wfob])
        k.act(wfo[:], wfo[:], AF.Exp, r=[wfob], w=[wfob])
        k.ts(wfo[:], wfo[:], sc, ALU.mult, r=[wfob], w=[wfob])
        k.memset(S_in[:], 0.0, w=[S_inb], eng="dve")
        k.dump("lg", lg[:], lgb, [P, 16], F32)
        k.dump("MT", MT[:], MTb, [P, 8, P], F32)

        k.release(mDt)
        k.ri = 0

        def rope(pt, pb, tab, tabb):
            i = k.ri % 2
            k.ri += 1
            ro, rob = rout[i]
            X, Xb = rX[0]
            Y, Yb = rY[0]
            tv = pt[:].rearrange("p (h two i) -> p h two i", h=4, two=2)
            cc = tab[:, 0:1, :].unsqueeze(1).to_broadcast([P, 4, 2, 64])
            ss = tab[:, 1:2, :].unsqueeze(1).to_broadcast([P, 4, 2, 64])
            k.tt(X[:], tv, cc, ALU.mult, r=[pb, tabb], w=[Xb])
            k.tt(Y[:], tv, ss, ALU.mult, r=[pb, tabb], w=[Yb])
            k.tt(ro[:, :, 0, :], X[:, :, 0, :], Y[:, :, 1, :], ALU.subtract, r=[Xb, Yb], w=[rob])
            k.tt(ro[:, :, 1, :], Y[:, :, 0, :], X[:, :, 1, :], ALU.add, r=[Xb, Yb], w=[rob])
            return ro, rob

        mD1 = k.mark()
        st = make_stage()
        load_gain(st, g_mix)
        hTo, _hTob0 = k.sb("hTo", [P, KC, TOK], BF16, at=oreg)
        hTob = [Buf("hTo%d" % i, k.inh) for i in range(8)]
        kfo, kfob = k.sb("kfo", [P, 2, 8, 8, P], BF16, at=oreg + 32768)
        vo = [k.sb("vo%d" % i, [P, 8, 512], BF16) for i in range(1)]
        ro_t = [k.sb("ro_t%d" % i, [P, 8, 2, 64], F32) for i in range(1)]
        for bi in range(3):
            rt, rtb = ro_t[0]
            k.dma("sp", rt[:], rope_oth_in[:, bi * 8:(bi + 1) * 8], w=[rtb], key=("rot", 0))
            def rms_o1(ti, bi=bi):
                g = bi * 8 + ti
                rms1(st, g, src_dram=x_oth[g * P:(g + 1) * P, :])

            def rms_o2(ti, bi=bi):
                g = bi * 8 + ti
                rms2(st, g, hTo, hTob[ti], ti * P)

            for cb in range(2):
                wA, wB = wpair(w_in, [(C_RK + cb * 512, 512)])
                pend = {}

                def p1(ti, wA=wA, wB=wB):
                    pt, pb = ps()
                    gemm_tok(pt[:], pb, hTo, hTob[ti], ti * P, wA, wB, 0, 512)
                    pend[ti] = (pt, pb)

                def p2(ti, cb=cb, bi=bi, rt=rt, rtb=rtb):
                    pt, pb = pend.pop(ti)
                    ro, rob = rope(pt, pb, rt[:, ti], rtb)
                    rv = ro[:].rearrange("p h two i -> p h (two i)")
                    for d_ in range(2):
                        k.tt(kfo[:, d_, ti, cb * 4:(cb + 1) * 4, :], rv,
                             wfo[:, d_, bi * 8 + ti, cb * 4:(cb + 1) * 4].unsqueeze(2).to_broadcast([P, 4, P]),
                             ALU.mult, r=[rob, wfob], w=[kfob])

                if cb == 0:
                    rms_o1(0)
                    rms_o1(1)
                    rms_o2(0)
                    for ti in range(8):
                        p1(ti)
                        if ti + 2 < 8:
                            rms_o1(ti + 2)
                        if ti + 1 < 8:
                            rms_o2(ti + 1)
                        if ti >= 1:
                            p2(ti - 1)
                    p2(7)
                else:
                    p1(0)
                    for ti in range(8):
                        if ti + 1 < 8:
                            p1(ti + 1)
                        p2(ti)
            for cb in range(4):
                wA, wB = wpair(w_in, [(C_RV + cb * 512, 512)])
                vt, vb = vo[0]
                for ti in range(8):
                    pt, pb = ps()
                    gemm_tok(pt[:], pb, hTo, hTob[ti], ti * P, wA, wB, 0, 512)
                    k.copy(vt[:, ti, :], pt[:], r=[pb], w=[vb], eng="act")
                for d_ in range(2):
                    pk, pkb = ps()
                    k.mmg([(pk[:, hl * 256:(hl + 1) * 256],
                            [(kfo[:, d_, ti, 2 * cb + hl, :], vt[:, ti, hl * 256:(hl + 1) * 256]) for ti in range(8)])
                           for hl in range(2)], r=[kfob, vb], w=[pkb])
                    sv = S_in[:, d_, 2 * cb:2 * cb + 2, :].rearrange("p h e -> p (h e)")
                    k.tt(sv, pk[:], sv, ALU.add, r=[pkb, S_inb], w=[S_inb])
        k.dump("S_in", S_in[:], S_inb, [P, 2, 8, 256], F32)
        k.release(mD1)

        o_retT, o_retTb = k.sb("o_retT", [P, 16, TOK], BF16, at=oreg)
        qkT, qkTb = k.sb("qkT", [P, 8, 8, P], BF16, at=oreg + 32768)
        k3, k3b = k.sb("k3", [P, 8, 3, 2, P], BF16, at=oreg + 49152)
        q3s = [k.sb("q3s%d" % i, [P, 3, 2, P], BF16) for i in range(2)]
        vv, vvb = k.sb("vv", [P, 8, 512], BF16)
        gs2 = [k.sb("gs2_%d" % i, [P, 512], BF16) for i in range(2)]
        gnb, gnbb = k.sb("gnb", [P, 512], F32)
        Sbf, Sbfb = k.sb("Sbf", [P, 2, 8, 256], BF16)
        sff = [[k.sb("sff%d_%d" % (hl, i), [P, 256], F32) for i in range(2)] for hl in range(2)]
        sbf32 = [[k.sb("sbf%d_%d" % (hl, i), [P, 256], F32) for i in range(2)] for hl in range(2)]
        sbb = [k.sb("sbb%d" % i, [P, 2, 256], BF16) for i in range(2)]
        Pm = [k.sb("Pm%d" % i, [P, 2, P], BF16) for i in range(2)]
        gst = [k.sb("gst%d" % i, [P, 2, 6], F32) for i in range(2)]
        gmv = [k.sb("gmv%d" % i, [P, 2, 5], F32) for i in range(2)]
        yb_ = [k.sb("yb%d" % i, [P, 256], F32) for i in range(2)]
        ot = [k.sb("ot%d" % i, [P, 512], BF16) for i in range(2)]
        for hp in range(4):
            k.dma("sp", gnb[:], gn_g[:, hp * 512:(hp + 1) * 512].partition_broadcast(P), w=[gnbb])
            wqkA, wqkB = wpair(w_in, [(C_RQ + hp * 256, 256), (C_RK + hp * 256, 256)])
            wvA, wvB = wpair(w_in, [(C_RV + hp * 512, 512)])
            pend = {}

            def q1(n):
                pt, pb = ps()
                gemm_tok(pt[:], pb, hT, hTb, n * P, wqkA, wqkB, 0, 512)
                pend[n] = (pt, pb)

            def q2(n):
                pt, pb = pend.pop(n)
                ro, rob = rope(pt, pb, rope_own[:, n], rope_ownb)
                rv = ro[:].rearrange("p h two i -> p h (two i)")
                q3, q3b = q3s[n % 2]
                k.tt(q3[:], rv[:, 0:2, :].unsqueeze(1).to_broadcast([P, 3, 2, P]),
                     wq3[:, :, 2 * hp:2 * hp + 2].unsqueeze(3).to_broadcast([P, 3, 2, P]), ALU.mult,
                     r=[rob, wq3b], w=[q3b])
                k.tt(k3[:, n], rv[:, 2:4, :].unsqueeze(1).to_broadcast([P, 3, 2, P]),
                     wk3[:, :, 2 * hp:2 * hp + 2].unsqueeze(3).to_broadcast([P, 3, 2, P]), ALU.mult,
                     r=[rob, wk3b], w=[k3b])
                ptT, ptTb = ps()
                pv = ptT[:].bitcast(BF16)
                lst = []
                for v in range(3):
                    for hl in range(2):
                        lst.append((pv[:, (v * 2 + hl) * P:(v * 2 + hl + 1) * P], q3[:, v, hl, :]))
                for hl in range(2):
                    lst.append((pv[:, (6 + hl) * P:(7 + hl) * P], k3[:, n, 0, hl, :]))
                k.tr(lst, ident[:], r=[q3b, k3b, identb], w=[ptTb])
                k.copy(qkT[:, n], pv.rearrange("p (c t) -> p c t", c=8), r=[ptTb], w=[qkTb], eng="act")

            q1(0)
            for n in range(8):
                if n + 1 < 8:
                    q1(n + 1)
                q2(n)
            cur = []
            for hl in range(2):
                h = 2 * hp + hl
                k.copy(Sbf[:, hl, 0, :], S_in[:, 0, h, :], r=[S_inb], w=[Sbfb], eng="act")
                cur.append((S_in[:, 0, h, :], S_inb))
            for n in range(8):
                pt, pb = ps()
                gemm_tok(pt[:], pb, hT, hTb, n * P, wvA, wvB, 0, 512)
                k.copy(vv[:, n, :], pt[:], r=[pb], w=[vvb], eng="act")
                if n < 7:
                    pk, pkb = ps()
                    k.mmg([(pk[:, hl * 256:(hl + 1) * 256], [(k3[:, n, 1, hl, :], vv[:, n, hl * 256:(hl + 1) * 256])])
                           for hl in range(2)], r=[k3b, vvb], w=[pkb])
                    for hl in range(2):
                        h = 2 * hp + hl
                        nx, nxb = sff[hl][n % 2]
                        k.stt(nx[:], cur[hl][0], gC[:, h:h + 1], pk[:, hl * 256:(hl + 1) * 256], ALU.mult, ALU.add,
                              r=[cur[hl][1], gCb, pkb], w=[nxb])
                        k.copy(Sbf[:, hl, n + 1, :], nx[:], r=[nxb], w=[Sbfb], eng="act")
                        cur[hl] = (nx[:], nxb)
            wgA, wgB = wpair(w_in, [(C_RG + hp * 512, 512)])
            curb = []
            for hl in range(2):
                h = 2 * hp + hl
                curb.append((S_in[:, 1, h, :], S_inb))
            live = {}

            def stA1(idx):
                n = 7 - idx
                sb_t, sb_b = sbb[idx % 2]
                for hl in range(2):
                    k.copy(sb_t[:, hl, :], curb[hl][0], r=[curb[hl][1]], w=[sb_b], eng="act")
                if n > 0:
                    pk, pkb = psb[4 + idx % 2]
                    k.mmg([(pk[:, hl * 256:(hl + 1) * 256], [(k3[:, n, 2, hl, :], vv[:, n, hl * 256:(hl + 1) * 256])])
                           for hl in range(2)], r=[k3b, vvb], w=[pkb])
                    live[("pk", idx)] = (pk, pkb)
                pst, pstb = psb[0 + idx % 2]
                k.mmg([(pst[:, hl * P:(hl + 1) * P], [(qkT[:, n, 6 + hl, :], qkT[:, n, hl, :])]) for hl in range(2)],
                      r=[qkTb], w=[pstb])
                pm, pmb = Pm[idx % 2]
                k.tt(pm[:], pst[:, 0:256].rearrange("p (h i) -> p h i", h=2), MT[:, 2 * hp:2 * hp + 2, :], ALU.mult,
                     r=[pstb, MTb], w=[pmb])
                po, pob = psb[2 + idx % 2]
                k.mmg([(po[:, hl * 256:(hl + 1) * 256],
                        [(pm[:, hl, :], vv[:, n, hl * 256:(hl + 1) * 256]),
                         (qkT[:, n, 2 + hl, :], Sbf[:, hl, n, :]),
                         (qkT[:, n, 4 + hl, :], sb_t[:, hl, :])]) for hl in range(2)],
                      r=[pmb, vvb, qkTb, Sbfb, sb_b], w=[pob])
                live[idx] = (po, pob)
                pg, pgb = psb[6 + idx % 2]
                gemm_tok(pg[:], pgb, hT, hTb, n * P, wgA, wgB, 0, 512)
                g2, g2b = gs2[idx % 2]
                k.act(g2[:], pg[:], AF.Silu, r=[pgb], w=[g2b])
                k.tt(g2[:], g2[:], gnb[:], ALU.mult, r=[g2b, gnbb], w=[g2b])

            def stA2(idx):
                n = 7 - idx
                if n > 0:
                    pk, pkb = live.pop(("pk", idx))
                    for hl in range(2):
                        h = 2 * hp + hl
                        nx, nxb = sbf32[hl][idx % 2]
                        k.stt(nx[:], curb[hl][0], gC[:, 8 + h:9 + h], pk[:, hl * 256:(hl + 1) * 256], ALU.mult, ALU.add,
                              r=[curb[hl][1], gCb, pkb], w=[nxb])
                        curb[hl] = (nx[:], nxb)

            def stB1(idx):
                po, pob = live[idx]
                gs, gsb = gst[idx % 2]
                gm, gmb = gmv[idx % 2]
                for hl in range(2):
                    k.s.op("dve", lambda h_, o=gs[:, hl, :], i=po[:, hl * 256:(hl + 1) * 256]: h_.bn_stats(out=o, in_=i),
                           r=[pob], w=[gsb])
                for hl in range(2):
                    k.s.op("dve", lambda h_, o=gm[:, hl, 0:2], i=gs[:, hl, :]: h_.bn_aggr(out=o, in_=i),
                           r=[gsb], w=[gmb])
                k.act(gm[:, :, 2], gm[:, :, 1], AF.Sqrt, r=[gmb], w=[gmb], bias=EPS)

            def stB2(idx):
                n = 7 - idx
                po, pob = live.pop(idx)
                gm, gmb = gmv[idx % 2]
                k.recip(gm[:, :, 3], gm[:, :, 2], r=[gmb], w=[gmb])
                k.stt(gm[:, :, 4], gm[:, :, 0], -1.0, gm[:, :, 3], ALU.mult, ALU.mult, r=[gmb], w=[gmb])
                o_t, o_b = ot[idx % 2]
                for hl in range(2):
                    y_t, y_b = yb_[hl]
                    k.act(y_t[:], po[:, hl * 256:(hl + 1) * 256], AF.Identity, r=[pob, gmb], w=[y_b],
                          scale=gm[:, hl, 3:4], bias=gm[:, hl, 4:5])
                    g2, g2b = gs2[idx % 2]
                    k.tt(o_t[:, hl * 256:(hl + 1) * 256], y_t[:], g2[:, hl * 256:(hl + 1) * 256], ALU.mult,
                         r=[y_b, g2b], w=[o_b])
                ptT, ptTb = psb[0 + idx % 2]
                pv = ptT[:].bitcast(BF16)
                k.tr([(pv[:, 512 + c * P:512 + (c + 1) * P], o_t[:, c * P:(c + 1) * P]) for c in range(4)], ident[:],
                     r=[o_b, identb], w=[ptTb])
                k.copy(o_retT[:, hp * 4:(hp + 1) * 4, n * P:(n + 1) * P],
                       pv[:, 512:1024].rearrange("p (c t) -> p c t", c=4), r=[ptTb], w=[o_retTb], eng="act")

            stA1(0)
            stA2(0)
            for idx in range(8):
                stB1(idx)
                if idx + 1 < 8:
                    stA1(idx + 1)
                stB2(idx)
                if idx + 1 < 8:
                    stA2(idx + 1)
        k.dump("o_retT", o_retT[:], o_retTb, [P, 16, TOK], BF16)
        k.release(mD)


    def qknorm(pr_list, prbs, inv_n, dst_list, dstb, gain_cols, gainb, sqs, rbs):
        n = pr_list[0].shape[-1]
        sq_used = []
        for i, (pr, prb) in enumerate(zip(pr_list, prbs)):
            sqt, sqb = sqs[(k.sqi + i) % len(sqs)]
            k.act(sqt[:, 0:n], pr, AF.Square, r=[prb], w=[sqb])
            sq_used.append((sqt, sqb))
        k.sqi += len(pr_list)
        pss, pssb = ps()
        k.mm(pss[:, 0:n], [(ones[:], sqt[:, 0:n]) for sqt, _ in sq_used], r=[onesb] + [b for _, b in sq_used], w=[pssb])
        rbt, rbb = rbs[k.rbi % len(rbs)]
        k.rbi += 1
        k.act(rbt[:, 0:n], pss[:, 0:n], AF.Ln, r=[pssb], w=[rbb], scale=inv_n, bias=EPS)
        k.act(rbt[:, 0:n], rbt[:, 0:n], AF.Exp, r=[rbb], w=[rbb], scale=-0.5)
        for pr, prb, dst, gc in zip(pr_list, prbs, dst_list, gain_cols):
            k.stt(dst, pr, gc, rbt[:, 0:n], ALU.mult, ALU.mult, r=[prb, gainb, rbb], w=[dstb])

    k.sqi = 0
    k.rbi = 0

    if on("C"):
        o_naT, o_naTb = k.sb("o_naT", [P, 8, TOK], BF16, at=oreg + 32768)
        mC = k.mark()
        hTh, hThb = k.sb("hTh", [P, KC, TOK], BF16)
        mst = k.mark()
        st = make_stage(3)
        load_gain(st, g_mix)
        rmsnorm_seq(st, 8, hTh, lambda i: hThb, lambda i: i * P, lambda i: dict(src_dram=x_halo[i * P:(i + 1) * P, :]))
        k.release(mst)
        nag, nagb = k.sb("nag", [P, 2], F32)
        k.dma("sp", nag[:], nag_in, w=[nagb])
        k.ts(nag[:, 0:1], nag[:, 0:1], float(P) ** -0.5, ALU.mult, r=[nagb], w=[nagb])
        qT, qTb = k.sb("qT", [P, 2, TOK], BF16)
        kT, kTb = k.sb("kT", [P, 2, 2048], BF16)
        v_e, v_eb = k.sb("v_e", [P, 16, 256], BF16)
        v_o, v_ob = k.sb("v_o", [P, 15, 256], BF16)
        Est, Estb = k.sb("Est", [P, 1024], F32)
        Etab = [k.sb("E%d" % i, [P, 8, 512], BF16, at=oreg + 49152 + i * 8192) for i in range(2)]
        sqs = [k.sb("sq%d" % i, [P, 512], BF16) for i in range(2)]
        rbs = [k.sb("rb%d" % i, [P, 512], F32) for i in range(2)]
        eS = [k.sb("eS%d" % i, [P, 512], BF16) for i in range(3)]
        PmN = [k.sb("PmN%d" % i, [P, 512], BF16) for i in range(3)]
        rden, rdenb = k.sb("rden", [P, 512], F32)

        def win_src(blk):
            if blk == 0:
                return (lambda kc: hTh[:, kc, 0:512]), hThb
            if blk == 1:
                return (lambda kc: hT[:, kc, 0:512]), hTb
            if blk == 2:
                return (lambda kc: hT[:, kc, 512:1024]), hTb
            return (lambda kc: hTh[:, kc, 512:1024]), hThb

        for hp in range(4):
            wqkA, wqkB = wpair(w_in, [(C_NAQ + hp * 256, 256), (C_NAK + hp * 256, 256)])
            wvA, wvB = wpair(w_in, [(C_NAV + hp * 256, 256)])
            def epiece(i):
                hl, qd = i // 4, i % 4
                h = 2 * hp + hl
                et, etb = Etab[hl]
                k.dma("sp", Est[:], nab_in[:, h, qd * 1024:(qd + 1) * 1024], w=[Estb], key="Est")
                k.act(et[:, qd * 2:(qd + 1) * 2, :].rearrange("p c f -> p (c f)"), Est[:], AF.Exp, r=[Estb], w=[etb])

            jobs = []
            for hl in range(2):
                for tb in range(2):
                    jobs.append((((lambda kc, tb=tb: hT[:, kc, tb * 512:(tb + 1) * 512]), hTb), hl * P,
                                 qT[:, hl, tb * 512:(tb + 1) * 512], qTb, nag[:, 0:1]))
                for blk in range(4):
                    jobs.append((win_src(blk), 256 + hl * P, kT[:, hl, blk * 512:(blk + 1) * 512], kTb, nag[:, 1:2]))
            pend = {}

            def j1(i):
                src, c0, dst, dstb, gcol = jobs[i]
                pr, prb = ps()
                gemm_feat(pr[:], prb, src, wqkA, wqkB, c0)
                sqt, sqb = sqs[i % 2]
                k.act(sqt[:], pr[:], AF.Square, r=[prb], w=[sqb])
                pend[i] = (pr, prb, sqt, sqb)

            def j2(i):
                src, c0, dst, dstb, gcol = jobs[i]
                pr, prb, sqt, sqb = pend.pop(i)
                pss, pssb = ps()
                k.mm(pss[:], [(ones[:], sqt[:])], r=[onesb, sqb], w=[pssb])
                rbt, rbb = rbs[i % 2]
                k.act(rbt[:], pss[:], AF.Ln, r=[pssb], w=[rbb], scale=1.0 / P, bias=EPS)
                k.act(rbt[:], rbt[:], AF.Exp, r=[rbb], w=[rbb], scale=-0.5)
                k.stt(dst, pr[:], gcol, rbt[:], ALU.mult, ALU.mult, r=[prb, nagb, rbb], w=[dstb])

            j1(0)
            for i in range(len(jobs)):
                if i + 1 < len(jobs):
                    j1(i + 1)
                j2(i)
                if i < 8:
                    epiece(i)
            for m in range(16):
                if m < 4:
                    src, srcb, t0_ = hTh, hThb, m * P
                elif m < 12:
                    src, srcb, t0_ = hT, hTb, (m - 4) * P
                else:
                    src, srcb, t0_ = hTh, hThb, 512 + (m - 12) * P
                pv_, pvb = ps()
                gemm_tok(pv_[:, 0:256], pvb, src, srcb, t0_, wvA, wvB, 0, 256)
                k.copy(v_e[:, m, :], pv_[:, 0:256], r=[pvb], w=[v_eb], eng="act")
            k.dma("sp", v_o[0:64, :, :], v_e[64:128, 0:15, :], r=[v_eb], w=[v_ob])
            k.dma("sp", v_o[64:128, :, :], v_e[0:64, 1:16, :], r=[v_eb], w=[v_ob])
            items = [(hl, r) for hl in range(2) for r in range(16)]
            state = {}
            k.ps_ring = [0, 1, 2, 3]

            def s1(i):
                hl, r = items[i]
                t0, t1 = na_tiles(r)
                nw = (t1 - t0) * 64
                pS, pSb = ps()
                k.mmg([(pS[:, (t - t0) * 64:(t - t0 + 1) * 64],
                        [(kT[:, hl, (r + 2 * t) * 64:(r + 2 * t) * 64 + P], qT[:, hl, r * 64:(r + 1) * 64])])
                       for t in range(t0, t1)], r=[kTb, qTb], w=[pSb])
                es, esb = eS[i % 3]
                k.act(es[:, 0:nw], pS[:, 0:nw], AF.Exp, r=[pSb], w=[esb])
                pm, pmb = PmN[i % 3]
                et, etb = Etab[hl]
                k.tt(pm[:, 0:nw], es[:, 0:nw], et[:, na_cls(r), t0 * 64:t1 * 64], ALU.mult, r=[esb, etb], w=[pmb])
                state[i] = (pm, pmb)

            def s2(i):
                hl, r = items[i]
                h = 2 * hp + hl
                pm, pmb = state.pop(i)
                r8 = r % 8
                if r8 == 0:
                    pair = state.get("pair", 0)
                    state["pair"] = pair + 1
                    state["O"] = psb[4 + 2 * (pair % 2)]
                    state["D"] = psb[5 + 2 * (pair % 2)]
                pO, pOb = state["O"]
                pD, pDb = state["D"]
                vt, vtb = (v_e, v_eb) if r % 2 == 0 else (v_o, v_ob)
                m0_ = r // 2
                t0, t1 = na_tiles(r)
                k.mmg([(pO[:, r8 * 64:(r8 + 1) * 64],
                        [(vt[:, m0_ + t, hl * P:(hl + 1) * P], pm[:, (t - t0) * 64:(t - t0 + 1) * 64]) for t in range(t0, t1)]),
                       (pD[:, r8 * 64:(r8 + 1) * 64],
                        [(ones[:], pm[:, (t - t0) * 64:(t - t0 + 1) * 64]) for t in range(t0, t1)])],
                      r=[vtb, pmb, onesb], w=[pOb, pDb])
                if r8 == 7:
                    k.act(rden[:], pD[:], AF.Ln, r=[pDb], w=[rdenb])
                    k.act(rden[:], rden[:], AF.Exp, r=[rdenb], w=[rdenb], scale=-1.0)
                    k.tt(o_naT[:, h, (r - 7) * 64:(r + 1) * 64], pO[:], rden[:], ALU.mult, r=[pOb, rdenb], w=[o_naTb])

            s1(0)
            s1(1)
            for i in range(len(items)):
                if i + 2 < len(items):
                    s1(i + 2)
                s2(i)
            k.ps_ring = list(range(8))
        k.dump("o_naT", o_naT[:], o_naTb, [P, 8, TOK], BF16)
        k.release(mC)

    if on("B"):
        o_memT, o_memTb = k.sb("o_memT", [P, 8, TOK], BF16, at=oreg + 49152)
        mB = k.mark()
        memT, memTb = k.sb("memT", [P, KC, 256], BF16)
        mst = k.mark()
        st = make_stage()
        load_gain(st, g_mem)
        for i in range(2):
            rmsnorm_tile(st, i, memT, memTb, i * P, src_dram=mem_in[i * P:(i + 1) * P, :])
        k.release(mst)
        xag, xagb = k.sb("xag", [P, 4], F32)
        k.dma("sp", xag[:], xag_in, w=[xagb])
        k.ts(xag[:, 0:2], xag[:, 0:2], 256.0 ** -0.5, ALU.mult, r=[xagb], w=[xagb])
        kxT, kxTb = k.sb("kxT", [P, 4, 2, 256], BF16)
        vx, vxb = k.sb("vx", [P, 2, 1024], BF16)
        sqs = [k.sb("sqx%d" % i, [P, 512], BF16) for i in range(4)]
        rbs = [k.sb("rbx%d" % i, [P, 512], F32) for i in range(2)]
        qx = [k.sb("qx%d" % i, [P, 2, 512], BF16) for i in range(2)]
        PT = [k.sb("PT%d" % i, [P, 2, 512], BF16) for i in range(2)]
        rdx = [k.sb("rdx%d" % i, [P, 512], F32) for i in range(2)]
        for hx2 in range(2):
            wA, wB = wpair(w_mkv, [(hx2 * 512, 512)])
            for hl in range(2):
                hx = 2 * hx2 + hl
                pr, prb = ps()
                for c in range(2):
                    gemm_feat(pr[:, c * 256:(c + 1) * 256], prb, ((lambda kc: memT[:, kc, :]), memTb), wA, wB, (hl * 2 + c) * P)
                qknorm([pr[:, 0:256], pr[:, 256:512]], [prb, prb], 1.0 / 256, [kxT[:, hx, 0, :], kxT[:, hx, 1, :]], kxTb,
                       [xag[:, 2:3], xag[:, 3:4]], xagb, sqs, rbs)
        for cb in range(2):
            wA, wB = wpair(w_mkv, [(1024 + cb * 512, 512)])
            for mt in range(2):
                pv_, pvb = ps()
                gemm_tok(pv_[:], pvb, memT, memTb, mt * P, wA, wB, 0, 512)
                k.copy(vx[:, mt, cb * 512:(cb + 1) * 512], pv_[:], r=[pvb], w=[vxb], eng="act")
        iters = [(hx2, hl, tb) for hx2 in range(2) for hl in range(2) for tb in range(2)]
        wts = {}
        pendx = {}

        def x1(i):
            hx2, hl, tb = iters[i]
            if hx2 not in wts:
                wts[hx2] = wpair(w_in, [(C_XQ + hx2 * 512, 512)])
            wA, wB = wts[hx2]
            prs = [psb[0 + 2 * (i % 2)], psb[1 + 2 * (i % 2)]]
            sq_used = []
            for c in range(2):
                gemm_feat(prs[c][0][:], prs[c][1], ((lambda kc, tb=tb: hT[:, kc, tb * 512:(tb + 1) * 512]), hTb),
                          wA, wB, (hl * 2 + c) * P)
            for c in range(2):
                sqt, sqb = sqs[(2 * i + c) % 4]
                k.act(sqt[:], prs[c][0][:], AF.Square, r=[prs[c][1]], w=[sqb])
                sq_used.append((sqt, sqb))
            pendx[i] = (prs, sq_used)

        def x2(i):
            hx2, hl, tb = iters[i]
            hx = 2 * hx2 + hl
            prs, sq_used = pendx.pop(i)
            pss, pssb = ps()
            k.mm(pss[:], [(ones[:], sqt[:]) for sqt, _ in sq_used], r=[onesb] + [b for _, b in sq_used], w=[pssb])
            rbt, rbb = rbs[i % 2]
            k.act(rbt[:], pss[:], AF.Ln, r=[pssb], w=[rbb], scale=1.0 / 256, bias=EPS)
            k.act(rbt[:], rbt[:], AF.Exp, r=[rbb], w=[rbb], scale=-0.5)
            qt, qb = qx[i % 2]
            for c in range(2):
                k.stt(qt[:, c, :], prs[c][0][:], xag[:, c:c + 1], rbt[:], ALU.mult, ALU.mult,
                      r=[prs[c][1], xagb, rbb], w=[qb])
            ptt, ptb = PT[i % 2]
            for mt in range(2):
                pS, pSb = ps()
                k.mm(pS[:], [(kxT[:, hx, c, mt * P:(mt + 1) * P], qt[:, c, :]) for c in range(2)], r=[kxTb, qb], w=[pSb])
                k.act(ptt[:, mt, :], pS[:], AF.Exp, r=[pSb], w=[ptb])
            pOs = [ps(), ps()]
            for c in range(2):
                k.mm(pOs[c][0][:], [(vx[:, mt, hx * 256 + c * P:hx * 256 + (c + 1) * P], ptt[:, mt, :]) for mt in range(2)],
                     r=[vxb, ptb], w=[pOs[c][1]])
            pD, pDb = ps()
            k.mm(pD[:], [(ones[:], ptt[:, mt, :]) for mt in range(2)], r=[onesb, ptb], w=[pDb])
            rd, rdb = rdx[i % 2]
            k.act(rd[:], pD[:], AF.Ln, r=[pDb], w=[rdb])
            k.act(rd[:], rd[:], AF.Exp, r=[rdb], w=[rdb], scale=-1.0)
            for c in range(2):
                k.tt(o_memT[:, hx * 2 + c, tb * 512:(tb + 1) * 512], pOs[c][0][:], rd[:], ALU.mult,
                     r=[pOs[c][1], rdb], w=[o_memTb])

        k.ps_ring = [4, 5, 6, 7]
        x1(0)
        for i in range(len(iters)):
            if i + 1 < len(iters):
                x1(i + 1)
            x2(i)
        k.ps_ring = list(range(8))
        k.dump("o_memT", o_memT[:], o_memTb, [P, 8, TOK], BF16)
        k.release(mB)

    if on("E"):
        mergedT, mergedTb = k.sb("mergedT", [P, KC, TOK], BF16)
        mE = k.mark()
        macc, maccb = k.sb("macc", [P, 4, TOK], F32)
        sig8 = [k.sb("sig8_%d" % i, [P, 8, 512], BF16) for i in range(2)]
        tmpE = [k.sb("tmpE%d" % i, [P, 512], F32) for i in range(2)]
        branches = [(C_GNA, w_bna, o_naT, o_naTb, 8), (C_GRET, w_bret, o_retT, o_retTb, 16), (C_GMEM, w_bmem, o_memT, o_memTb, 8)]
        it = 0
        gi = 0
        for cb in range(4):
            for bi_, (cg, wbr, oT, oTb, nk) in enumerate(branches):
                gA, gB = wpair(w_in, [(cg + cb * 512, 512)])
                sg8, sg8b = sig8[gi % 2]
                gi += 1
                for ct in range(4):
                    for tb in range(2):
                        pg, pgb = ps()
                        gemm_feat(pg[:], pgb, ((lambda kc, tb=tb: hT[:, kc, tb * 512:(tb + 1) * 512]), hTb), gA, gB, ct * P)
                        k.act(sg8[:, ct * 2 + tb, :], pg[:], AF.Sigmoid, r=[pgb], w=[sg8b])
                bA = wload(wbr, 0, [(cb * 512, 512)])
                bB = wload(wbr, 1024, [(cb * 512, 512)]) if nk == 16 else None
                for ct in range(4):
                    for tb in range(2):
                        pbr, pbrb = ps()
                        k.mm(pbr[:], [(bA[0][:, kc, ct * P:(ct + 1) * P], oT[:, kc, tb * 512:(tb + 1) * 512]) for kc in range(8)],
                             r=[oTb, bA[1]], w=[pbrb], start=True, stop=(bB is None))
                        if bB is not None:
                            k.mm(pbr[:], [(bB[0][:, kc, ct * P:(ct + 1) * P], oT[:, 8 + kc, tb * 512:(tb + 1) * 512]) for kc in range(8)],
                                 r=[oTb, bB[1]], w=[pbrb], start=False, stop=True)
                        sgs = sg8[:, ct * 2 + tb, :]
                        mslice = macc[:, ct, tb * 512:(tb + 1) * 512]
                        if bi_ == 0:
                            k.tt(mslice, pbr[:], sgs, ALU.mult, r=[pbrb, sg8b], w=[maccb])
                        else:
                            tm, tmb = tmpE[it % 2]
                            k.tt(tm[:], pbr[:], sgs, ALU.mult, r=[pbrb, sg8b], w=[tmb])
                            if bi_ == 1:
                                k.tt(mslice, mslice, tm[:], ALU.add, r=[maccb, tmb], w=[maccb])
                            else:
                                k.tt(mergedT[:, cb * 4 + ct, tb * 512:(tb + 1) * 512], mslice, tm[:], ALU.add,
                                     r=[maccb, tmb], w=[mergedTb])
                        it += 1
        k.dump("mergedT", mergedT[:], mergedTb, [P, KC, TOK], BF16)
        k.release(mE)

    if on("F"):
        k.release(mE)
        x1, x1b = k.sb("x1", [P, NT, D], F32, at=oreg)
        for ti in range(NT):
            k.dma("sp", x1[:, ti, :], x_own[ti * P:(ti + 1) * P, :], w=[x1b], key="x1ld")
        for cb in range(4):
            wA, wB = wpair(w_out, [(cb * 512, 512)])
            for ti in range(NT):
                pt, pb = ps()
                gemm_tok(pt[:], pb, mergedT, mergedTb, ti * P, wA, wB, 0, 512)
                xs_ = x1[:, ti, cb * 512:(cb + 1) * 512]
                k.tt(xs_, pt[:], xs_, ALU.add, r=[pb, x1b], w=[x1b])
        k.dump("x1", x1[:], x1b, [P, NT, D], F32)
        st = make_stage()
        load_gain(st, g_ffn)
        rmsnorm_seq(st, NT, hT, lambda i: hTb, lambda i: i * P, lambda i: dict(src_sb=(x1[:, i, :], x1b)))

    if on("G"):
        k.release(mE)
        uT = [k.sb("uT%d" % i, [P, 8, TOK], BF16) for i in range(2)]
        sqf = [k.sb("sqf%d" % i, [P, 512], F32) for i in range(2)]
        it = 0
        for fb in range(8):
            ut, utb = uT[fb % 2]
            for cbk in range(2):
                wA, wB = wpair(w_ff1, [(fb * 1024 + cbk * 512, 512)])
                for ct in range(4):
                    for tb in range(2):
                        pu, pub = ps()
                        gemm_feat(pu[:], pub, ((lambda kc, tb=tb: hT[:, kc, tb * 512:(tb + 1) * 512]), hTb), wA, wB, ct * P)
                        sq_, sqb_ = sqf[it % 2]
                        k.act(sq_[:], pu[:], AF.Square, r=[pub], w=[sqb_])
                        k.stt(ut[:, cbk * 4 + ct, tb * 512:(tb + 1) * 512], pu[:], 0.0, sq_[:], ALU.is_gt, ALU.mult,
                              r=[pub, sqb_], w=[utb])
                        it += 1
            for cb in range(4):
                wU = wload(w_ff2, fb * 1024, [(cb * 512, 512)])
                for ti in range(NT):
                    pt, pb = ps()
                    k.mm(pt[:], [(ut[:, kc, ti * P:(ti + 1) * P], wU[0][:, kc, :]) for kc in range(8)], r=[utb, wU[1]], w=[pb])
                    xs_ = x1[:, ti, cb * 512:(cb + 1) * 512]
                    k.tt(xs_, pt[:], xs_, ALU.add, r=[pb, x1b], w=[x1b])
        for ti in range(NT):
            oid = k.dma("sp", y_out[ti * P:(ti + 1) * P, :], x1[:, ti, :], r=[x1b], key="yst")
            s.final_waits.append(oid)

    with nc.Block() as block:
        s.emit(block)
    return k


def _rope_tab(pos):
    inv = np.power(np.float32(10000.0), -np.arange(64, dtype=np.float32) / np.float32(64)).astype(np.float32)
    ang = (pos.astype(np.float32)[:, None] * inv[None, :]).astype(np.float32)
    return np.stack([np.cos(ang), np.sin(ang)], axis=1).astype(np.float32)


def _consts(j):
    c = {}
    c["ident_in"] = np.eye(P, dtype=np.float32)
    own0 = j * TOK
    pos = own0 + np.arange(TOK)
    c["rope_own"] = np.ascontiguousarray(_rope_tab(pos).reshape(8, P, 2, 64).transpose(1, 0, 2, 3))
    pos_o, ef, eb = [], [], []
    for s_ in range(3):
        q = (j + 1 + s_) % 4
        pg = q * TOK + np.arange(TOK)
        pos_o.append(pg)
        ef.append(np.where(q < j, own0 - 1 - pg, BIG))
        eb.append(np.where(q > j, pg - (own0 + TOK), BIG))
    pos_o = np.concatenate(pos_o)
    c["rope_oth"] = np.ascontiguousarray(_rope_tab(pos_o).reshape(24, P, 2, 64).transpose(1, 0, 2, 3))
    ef = np.concatenate(ef).reshape(24, P).T
    eb = np.concatenate(eb).reshape(24, P).T
    c["e_oth"] = np.ascontiguousarray(np.stack([ef, eb], axis=1).astype(np.float32))
    a = np.arange(P, dtype=np.float32)
    c["e_own"] = np.stack([a + 1, P - a, P - 1 - a, a], axis=1).astype(np.float32)
    jj = np.arange(P)[:, None]
    ii = np.arange(P)[None, :]
    c["dmask"] = np.stack([np.maximum(ii - jj, 0), (jj <= ii), np.maximum(jj - ii, 0), (jj > ii)],
                          axis=1).astype(np.float32)
    return c


def host_inputs(inp, core):
    b, j = core // 4, core % 4
    m = dict(_consts(j))
    x = inp["x"]
    m["x_own"] = np.ascontiguousarray(x[b, j * TOK:(j + 1) * TOK])
    m["x_oth"] = np.ascontiguousarray(np.concatenate(
        [x[b, ((j + 1 + s_) % 4) * TOK:((j + 1 + s_) % 4 + 1) * TOK] for s_ in range(3)], axis=0))
    halo = np.zeros((TOK, D), np.float32)
    t0 = (16 * j - 8) * 64
    if t0 >= 0:
        halo[0:512] = x[b, t0:t0 + 512]
    t1 = (16 * j + 16) * 64
    if t1 + 512 <= 4096:
        halo[512:1024] = x[b, t1:t1 + 512]
    m["x_halo"] = halo
    m["mem_b"] = np.ascontiguousarray(inp["mem"][b])
    m["g_mix"] = np.ascontiguousarray(inp["norm_mix_g"][0][None, :])
    m["g_ffn"] = np.ascontiguousarray(inp["norm_ffn_g"][0][None, :])
    m["g_mem"] = np.ascontiguousarray(inp["mem_norm_g"][0][None, :])
    m["gn_g"] = np.ascontiguousarray(inp["ret_gn_g"][0][None, :])
    m["dlog"] = np.concatenate([inp["ret_decay_logit_fwd"][0], inp["ret_decay_logit_bwd"][0]])[None, :].astype(np.float32)
    m["nag"] = np.ascontiguousarray(np.stack([inp["na_q_norm_g"][0], inp["na_k_norm_g"][0]], axis=1))
    m["xag"] = np.ascontiguousarray(np.concatenate(
        [inp["xa_q_norm_g"][0].reshape(2, P).T, inp["xa_k_norm_g"][0].reshape(2, P).T], axis=1))
    m["na_bias"] = _na_bias(inp["na_rpb"][0], j)
    m["w_in"] = inp["w_in"][0]
    m["w_mem_kv"] = inp["w_mem_kv"][0]
    m["w_br_na"] = inp["w_br_na"][0]
    m["w_br_ret"] = inp["w_br_ret"][0]
    m["w_br_mem"] = inp["w_br_mem"][0]
    m["w_out"] = inp["w_out"][0]
    m["w_ff1"] = inp["w_ff1"][0]
    m["w_ff2"] = inp["w_ff2"][0]
    return m


NA_CLS_ROWS = [None, 0, 1, 2, 3, 13, 14, 15]


def na_cls(r):
    if 4 <= r <= 12:
        return 0
    return 1 + r if r < 4 else 5 + (r - 13)


def na_tiles(r):
    if r <= 1:
        return 2, 8
    if r <= 3:
        return 2, 7
    if r <= 12:
        return 2, 6
    if r <= 14:
        return 1, 6
    return 0, 6


def _na_bias(rpb, j):
    half = np.arange(2)[:, None, None, None, None]
    kc = np.arange(64)[None, :, None, None, None]
    cl = np.arange(8)[None, None, :, None, None]
    t = np.arange(8)[None, None, None, :, None]
    qc = np.arange(64)[None, None, None, None, :]
    rloc = np.array([6, 0, 1, 2, 3, 13, 14, 15])[cl]
    g = 16 * j + rloc
    rel = -8 + 2 * t + half
    start = np.clip(g - 4, 0, 56)
    krow = g + rel
    vrow = (krow >= start) & (krow <= start + 7)
    cstart = np.clip(qc - 8, 0, 48)
    vcol = (kc >= cstart) & (kc < cstart + 16)
    valid = np.broadcast_to(vrow & vcol, (2, 64, 8, 8, 64))
    dr = np.broadcast_to(np.clip(rel + 7, 0, 14), (2, 64, 8, 8, 64))
    dc = np.broadcast_to(np.clip(kc - qc, -15, 15) + 15, (2, 64, 8, 8, 64))
    out = np.empty((2, 64, 8, 8, 8, 64), np.float32)
    for h in range(8):
        out[:, :, h] = np.where(valid, rpb[h][dr, dc], np.float32(NEG))
    return np.ascontiguousarray(out.reshape(P, 8, 4096))


_CACHE = {}


def kernel(**inputs):
    inp = {k_: np.asarray(v) for k_, v in inputs.items()}
    if "k" not in _CACHE:
        _CACHE["k"] = build()
    k = _CACHE["k"]
    in_maps = []
    for c in range(8):
        m = host_inputs(inp, c)
        in_maps.append({n: np.ascontiguousarray(m[n], dtype=np.float32) for n in k.din})
    res = run_bass_kernel_spmd(k.nc, in_maps, core_ids=list(range(8)))
    out = np.empty((2, 4096, D), np.float32)
    for c in range(8):
        b, j = c // 4, c % 4
        out[b, j * TOK:(j + 1) * TOK] = res.results[c]["y_own"]
    return out
```

```python
import numpy as np
import concourse.bass as bass
import concourse.mybir as mybir
from concourse.bass_utils import run_bass_kernel_spmd

F32 = mybir.dt.float32
BF16 = mybir.dt.bfloat16
AF = mybir.ActivationFunctionType
ALU = mybir.AluOpType

ENGS = ("pe", "act", "dve", "pool", "sp")

P = 128
D = 2048
TOK = 1024
NT = TOK // P
KC = D // P
EPS = 1e-6
DFF = 8192
C_NAQ, C_NAK, C_NAV, C_RQ, C_RK, C_RV, C_RG, C_XQ, C_GNA, C_GRET, C_GMEM = (
    0, 1024, 2048, 3072, 4096, 5120, 7168, 9216, 10240, 12288, 14336)
NSLOT = 4
BIG = 1.0e6
NEG = -30000.0


class Buf:
    __slots__ = ("name", "lw", "rd", "inh")

    def __init__(self, name, inh=None):
        self.name = name
        self.lw = None
        self.rd = {}
        self.inh = dict(inh) if inh else None


class Sched:
    def __init__(self, nc):
        self.nc = nc
        self.ops = []
        self.eng_ops = {e: [] for e in ENGS}
        self.dma_sems = {}
        self.final_waits = []

    def op(self, eng, fn, r=(), w=(), dma_key=None):
        deps = {}

        def add(i, kind):
            if i is None:
                return
            old = deps.get(i)
            if old is None or (old == "war" and kind != "war"):
                deps[i] = kind

        for b in r:
            if b.inh:
                for i in b.inh:
                    add(i, "raw")
            add(b.lw, "raw")
        for b in w:
            if b.inh:
                for i in b.inh:
                    add(i, "raw")
                b.inh = None
            add(b.lw, "waw")
            for i in b.rd.values():
                add(i, "war")
        oid = len(self.ops)
        rec = dict(id=oid, eng=eng, fn=fn, deps=deps, dma=dma_key, pos=len(self.eng_ops[eng]),
                   sig=False, ticket=None)
        self.ops.append(rec)
        self.eng_ops[eng].append(rec)
        for b in r:
            if dma_key is not None:
                b.rd[("d", oid)] = oid
            else:
                b.rd[eng] = oid
        for b in w:
            b.lw = oid
            b.rd = {}
        return oid

    def barrier_deps(self):
        d = {}
        for e in ENGS:
            for rec in reversed(self.eng_ops[e]):
                if rec["dma"] is None:
                    d[rec["id"]] = "raw"
                    break
        latest = {}
        for rec in self.ops:
            if rec["dma"] is not None:
                latest[rec["dma"]] = rec["id"]
        for i in latest.values():
            d[i] = "raw"
        return d

    def _needed(self, rec, dep):
        kind = rec["deps"][dep["id"]]
        if dep["dma"] is not None or rec["dma"] is not None:
            return True
        if dep["eng"] != rec["eng"]:
            return True
        if rec["eng"] == "pe" or kind == "war":
            return False
        return (rec["pos"] - dep["pos"]) <= 3

    def emit(self, block):
        nc = self.nc
        ops = self.ops
        for rec in ops:
            for d in rec["deps"]:
                dep = ops[d]
                if dep["dma"] is None and self._needed(rec, dep):
                    dep["sig"] = True
        eng_sem = {e: nc.alloc_semaphore("sem_" + e) for e in ENGS}
        cnt = {e: 0 for e in ENGS}
        for e in ENGS:
            for rec in self.eng_ops[e]:
                if rec["dma"] is not None:
                    ent = self.dma_sems.get(rec["dma"])
                    if ent is None:
                        ent = [nc.alloc_semaphore("dsem_%d" % len(self.dma_sems)), 0]
                        self.dma_sems[rec["dma"]] = ent
                    ent[1] += 16
                    rec["sem"] = ent[0]
                    rec["ticket"] = ent[1]
                    if isinstance(rec["dma"], str) and rec["dma"].startswith("G:"):
                        ent.append(rec)
                elif rec["sig"]:
                    cnt[e] += 1
                    rec["sem"] = eng_sem[e]
                    rec["ticket"] = cnt[e]
        self.counts = cnt
        for ent in self.dma_sems.values():
            for rec in ent[2:]:
                rec["ticket"] = ent[1]

        def make(e):
            def body(h):
                waited = {}
                for rec in self.eng_ops[e]:
                    for d in sorted(rec["deps"]):
                        dep = ops[d]
                        if not self._needed(rec, dep):
                            continue
                        key = id(dep["sem"])
                        if waited.get(key, 0) >= dep["ticket"]:
                            continue
                        h.wait_ge(dep["sem"], dep["ticket"])
                        waited[key] = dep["ticket"]
                    ins = rec["fn"](h)
                    if rec["dma"] is not None:
                        ins.then_inc(rec["sem"], 16)
                    elif rec["sig"]:
                        ins.then_inc(rec["sem"], 1)
                if e == "sp":
                    for oid in self.final_waits:
                        rr = ops[oid]
                        h.wait_ge(rr["sem"], rr["ticket"])
            return body

        block.tensor(make("pe"))
        block.scalar(make("act"))
        block.vector(make("dve"))
        block.gpsimd(make("pool"))
        block.sync(make("sp"))


_DTSZ = {F32: 4, BF16: 2}


class K:
    def __init__(self, dbg=()):
        self.nc = nc = bass.Bass("TRN2", target_bir_lowering=False)
        self.s = Sched(nc)
        self.dbg = set(dbg)
        self.din = {}
        self.uid = 0
        self.ap_ptr = (int(nc.sbuf_base) + 63) // 64 * 64
        self.ap_top = int(nc.sbuf_top)
        self.inh = None
        self.nalloc = 0

    def dram_in(self, name, shape, dt=F32):
        t = self.nc.dram_tensor(name, list(shape), dt, kind="ExternalInput")
        self.din[name] = t
        return t.ap()

    def sb(self, name, shape, dt, at=None):
        n = 1
        for v in shape[1:]:
            n *= v
        nbytes = n * _DTSZ[dt]
        if at is None:
            off = self.ap_ptr
            self.ap_ptr += (nbytes + 63) // 64 * 64
            assert self.ap_ptr <= self.ap_top, ("SBUF overflow", name, self.ap_ptr, self.ap_top)
        else:
            off = at
        self.nalloc += 1
        t = self.nc.alloc_sbuf_tensor_at("%s_%d" % (name, self.nalloc), list(shape), dt, offset=off)
        return t, Buf(name, self.inh)

    def mark(self):
        return self.ap_ptr

    def release(self, m):
        self.ap_ptr = m
        self.inh = self.s.barrier_deps()

    def dma(self, q, out, in_, r=(), w=(), key=None):
        if key is None:
            self.uid += 1
            key = ("u", self.uid)
        return self.s.op(q, lambda h, o=out, i=in_: h.dma_start(out=o, in_=i), r=r, w=w, dma_key=key)

    def act(self, out, in_, func, r=(), w=(), scale=None, bias=None, accum_out=None):
        kw = {}
        if scale is not None:
            kw["scale"] = scale
        if bias is not None:
            kw["bias"] = bias
        if accum_out is not None:
            kw["accum_out"] = accum_out
        return self.s.op("act", lambda h: h.activation(out=out, in_=in_, func=func, **kw), r=r, w=w)

    def tt(self, out, in0, in1, op, r=(), w=(), eng="dve"):
        return self.s.op(eng, lambda h: h.tensor_tensor(out=out, in0=in0, in1=in1, op=op), r=r, w=w)

    def ts(self, out, in0, s1, op0, s2=None, op1=None, r=(), w=(), eng="dve"):
        if op1 is None:
            return self.s.op(eng, lambda h: h.tensor_scalar(out=out, in0=in0, scalar1=s1, scalar2=None,
                                                            op0=op0), r=r, w=w)
        return self.s.op(eng, lambda h: h.tensor_scalar(out=out, in0=in0, scalar1=s1, scalar2=s2,
                                                        op0=op0, op1=op1), r=r, w=w)

    def stt(self, out, in0, scalar, in1, op0, op1, r=(), w=()):
        return self.s.op("dve", lambda h: h.scalar_tensor_tensor(out=out, in0=in0, scalar=scalar, in1=in1,
                                                                 op0=op0, op1=op1), r=r, w=w)

    def copy(self, out, in_, r=(), w=(), eng="dve"):
        if eng == "act":
            return self.act(out, in_, AF.Copy, r=r, w=w)
        return self.s.op(eng, lambda h: h.tensor_copy(out=out, in_=in_), r=r, w=w)

    def recip(self, out, in_, r=(), w=()):
        return self.s.op("dve", lambda h: h.reciprocal(out=out, in_=in_), r=r, w=w)

    def memset(self, ap, val, w=(), eng="dve"):
        return self.s.op(eng, lambda h: h.memset(ap, val), w=w)

    def mm(self, out, pairs, r=(), w=(), start=True, stop=True):
        def fn(h):
            n = len(pairs)
            ins = None
            for i, (l, rh) in enumerate(pairs):
                ins = h.matmul(out, lhsT=l, rhs=rh, start=(start and i == 0), stop=(stop and i == n - 1))
            return ins
        return self.s.op("pe", fn, r=r, w=w)

    def mmg(self, groups, r=(), w=()):
        def fn(h):
            ins = None
            for out, pairs in groups:
                n = len(pairs)
                for i, (l, rh) in enumerate(pairs):
                    ins = h.matmul(out, lhsT=l, rhs=rh, start=(i == 0), stop=(i == n - 1))
            return ins
        return self.s.op("pe", fn, r=r, w=w)

    def tr(self, outs_ins, ident, r=(), w=()):
        def fn(h):
            ins = None
            for o, i in outs_ins:
                ins = h.transpose(out=o, in_=i, identity=ident)
            return ins
        return self.s.op("pe", fn, r=r, w=w)

    def dump(self, name, ap, buf, shape, dt):
        if name not in self.dbg:
            return
        o = self.nc.dram_tensor("dbg_" + name, list(shape), dt, kind="ExternalOutput").ap()
        oid = self.dma("sp", o, ap, r=[buf])
        self.s.final_waits.append(oid)


PHASES = "ADCBEFG"


def build(dbg=(), upto="G"):
    k = K(dbg)
    nc = k.nc
    s = k.s
    last = PHASES.index(upto)

    def on(ph):
        return PHASES.index(ph) <= last

    x_own = k.dram_in("x_own", [TOK, D])
    x_oth = k.dram_in("x_oth", [3 * TOK, D])
    x_halo = k.dram_in("x_halo", [TOK, D])
    mem_in = k.dram_in("mem_b", [256, D])
    g_mix = k.dram_in("g_mix", [1, D])
    g_ffn = k.dram_in("g_ffn", [1, D])
    g_mem = k.dram_in("g_mem", [1, D])
    gn_g = k.dram_in("gn_g", [1, D])
    ident_in = k.dram_in("ident_in", [P, P])
    rope_own_in = k.dram_in("rope_own", [P, 8, 2, 64])
    rope_oth_in = k.dram_in("rope_oth", [P, 24, 2, 64])
    e_oth_in = k.dram_in("e_oth", [P, 2, 24])
    e_own_in = k.dram_in("e_own", [P, 4])
    dmask_in = k.dram_in("dmask", [P, 4, P])
    dlog_in = k.dram_in("dlog", [1, 16])
    nag_in = k.dram_in("nag", [P, 2])
    xag_in = k.dram_in("xag", [P, 4])
    nab_in = k.dram_in("na_bias", [P, 8, 4096])
    w_in = k.dram_in("w_in", [D, 16384])
    w_mkv = k.dram_in("w_mem_kv", [D, 2048])
    w_bna = k.dram_in("w_br_na", [1024, D])
    w_bret = k.dram_in("w_br_ret", [2048, D])
    w_bmem = k.dram_in("w_br_mem", [1024, D])
    w_out = k.dram_in("w_out", [D, D])
    w_ff1 = k.dram_in("w_ff1", [D, DFF])
    w_ff2 = k.dram_in("w_ff2", [DFF, D])
    y_out = nc.dram_tensor("y_own", [TOK, D], F32, kind="ExternalOutput").ap()

    psb = []
    for i in range(8):
        t = nc.alloc_psum_tensor("ps%d" % i, [P, 512], F32)
        psb.append((t, Buf("ps%d" % i)))
    k.psi = 0

    k.ps_ring = list(range(8))

    def ps():
        t, b = psb[k.ps_ring[k.psi % len(k.ps_ring)]]
        k.psi += 1
        return t, b

    ident_f, ident_fb = k.sb("ident_f", [P, P], F32)
    ident, identb = k.sb("ident", [P, P], BF16)
    ones, onesb = k.sb("ones", [P, P], BF16)
    k.dma("sp", ident_f[:], ident_in, w=[ident_fb], key="G:c")
    k.copy(ident[:], ident_f[:], r=[ident_fb], w=[identb])
    k.memset(ones[:], 1.0, w=[onesb], eng="dve")
    wring = [k.sb("wr%d" % i, [P, 8, 512], BF16) for i in range(NSLOT)]
    k.wi = 0
    hT, hTb = k.sb("hT", [P, KC, TOK], BF16)

    def wload(w_dram, r0, segs):
        slot = k.wi % NSLOT
        t, b = wring[slot]
        k.wi += 1
        o = 0
        for (c0, ncols) in segs:
            src = w_dram[r0:r0 + 1024, c0:c0 + ncols].rearrange("(k p) c -> p k c", p=P)
            k.dma("pool", t[:, :, o:o + ncols], src, w=[b], key=("w", slot))
            o += ncols
        return t, b

    def wpair(w_dram, segs):
        return wload(w_dram, 0, segs), wload(w_dram, 1024, segs)

    class Stage:
        pass

    def make_stage(nxs=2):
        st = Stage()
        st.xs = [k.sb("xs%d" % i, [P, D], F32) for i in range(nxs)]
        st.xn = [k.sb("xn%d" % i, [P, D], BF16) for i in range(2)]
        st.st = [k.sb("st%d" % i, [P, 4], F32) for i in range(2)]
        st.gbc, st.gbcb = k.sb("gbc", [P, D], F32)
        return st

    def load_gain(st, g_dram):
        k.dma("sp", st.gbc[:], g_dram.partition_broadcast(P), w=[st.gbcb])

    def rms1(st, i, src_dram=None, src_sb=None):
        nt, nb = st.xn[i % 2]
        stt_, stb = st.st[i % 2]
        if src_sb is None:
            xt, xb = st.xs[i % len(st.xs)]
            k.dma("sp", xt[:], src_dram, w=[xb], key=("xs", i % len(st.xs)))
            xa = xt[:]
        else:
            xa, xb = src_sb
        k.act(nt[:], xa, AF.Square, r=[xb], w=[nb, stb], accum_out=stt_[:, 0:1], scale=float(D) ** -0.5)
        k.act(stt_[:, 1:2], stt_[:, 0:1], AF.Sqrt, r=[stb], w=[stb], bias=EPS)
        k.recip(stt_[:, 2:3], stt_[:, 1:2], r=[stb], w=[stb])
        k.stt(nt[:], xa, stt_[:, 2:3], st.gbc[:], ALU.mult, ALU.mult, r=[xb, stb, st.gbcb], w=[nb])

    def rms2(st, i, dstT, dstTb, col0):
        nt, nb = st.xn[i % 2]
        for half in range(2):
            pt, pb = ps()
            pv = pt[:].bitcast(BF16)
            k.tr([(pv[:, c * P:(c + 1) * P], nt[:, (half * 8 + c) * P:(half * 8 + c + 1) * P]) for c in range(8)],
                 ident[:], r=[nb, identb], w=[pb])
            k.copy(dstT[:, half * 8:(half + 1) * 8, col0:col0 + P],
                   pv.rearrange("p (c t) -> p c t", c=8), r=[pb], w=[dstTb],
                   eng=("act" if half == 0 else "dve"))

    def rmsnorm_tile(st, i, dstT, dstTb, col0, src_dram=None, src_sb=None):
        rms1(st, i, src_dram=src_dram, src_sb=src_sb)
        rms2(st, i, dstT, dstTb, col0)

    def rmsnorm_seq(st, n, dstT, dstTb_of, col_of, src_of):
        rms1(st, 0, **src_of(0))
        for i in range(n):
            if i + 1 < n:
                rms1(st, i + 1, **src_of(i + 1))
            rms2(st, i, dstT, dstTb_of(i), col_of(i))

    def gemm_tok(pout, pb, actT, actb, tok0, wA, wB, c0, ncols, kA=8, kB=8):
        (ta, ba), (tb_, bb) = wA, wB
        k.mm(pout, [(actT[:, kc, tok0:tok0 + P], ta[:, kc, c0:c0 + ncols]) for kc in range(kA)],
             r=[actb, ba], w=[pb], start=True, stop=(kB == 0))
        if kB:
            k.mm(pout, [(actT[:, 8 + kc, tok0:tok0 + P], tb_[:, kc, c0:c0 + ncols]) for kc in range(kB)],
                 r=[actb, bb], w=[pb], start=False, stop=True)

    def gemm_feat(pout, pb, srcs, wA, wB, c0):
        (ta, ba), (tb_, bb) = wA, wB
        af, actb = srcs
        k.mm(pout, [(ta[:, kc, c0:c0 + P], af(kc)) for kc in range(8)], r=[actb, ba], w=[pb],
             start=True, stop=False)
        k.mm(pout, [(tb_[:, kc, c0:c0 + P], af(8 + kc)) for kc in range(8)], r=[actb, bb], w=[pb],
             start=False, stop=True)

    m0 = k.mark()
    st = make_stage(4)
    load_gain(st, g_mix)
    rmsnorm_seq(st, NT, hT, lambda i: hTb, lambda i: i * P, lambda i: dict(src_dram=x_own[i * P:(i + 1) * P, :]))
    k.dump("hT", hT[:], hTb, [P, KC, TOK], BF16)

    if on("D"):
        k.release(m0)
        oreg = k.mark()
        k.ap_ptr += 65536
        mD = k.mark()
        lg, lgb = k.sb("lg", [P, 16], F32)
        eown, eownb = k.sb("eown", [P, 4], F32)
        wq3, wq3b = k.sb("wq3", [P, 3, 8], F32)
        wk3, wk3b = k.sb("wk3", [P, 3, 8], F32)
        gC, gCb = k.sb("gC", [P, 16], F32)
        MT, MTb = k.sb("MT", [P, 8, P], F32)
        eo, eob = k.sb("eo", [P, 2, 24], F32)
        wfo, wfob = k.sb("wfo", [P, 2, 24, 8], F32)
        S_in, S_inb = k.sb("S_in", [P, 2, 8, 256], F32)
        rope_own, rope_ownb = k.sb("rope_own", [P, 8, 2, 64], F32)
        rout = [k.sb("rout%d" % i, [P, 4, 2, 64], F32) for i in range(2)]
        rX = [k.sb("rX%d" % i, [P, 4, 2, 64], F32) for i in range(1)]
        rY = [k.sb("rY%d" % i, [P, 4, 2, 64], F32) for i in range(1)]
        mDt = k.mark()
        dl, dlb = k.sb("dl", [P, 16], F32)
        dm, dmb = k.sb("dm", [P, 4, P], F32)
        mtmp, mtmpb = k.sb("mtmp", [P, 2, P], F32)
        k.dma("sp", dl[:], dlog_in.partition_broadcast(P), w=[dlb])
        k.dma("sp", eown[:], e_own_in, w=[eownb])
        k.dma("sp", dm[:], dmask_in, w=[dmb])
        k.dma("sp", eo[:], e_oth_in, w=[eob])
        k.dma("sp", rope_own[:], rope_own_in, w=[rope_ownb])
        sc = float(P) ** -0.5
        k.act(lg[:], dl[:], AF.Exp, r=[dlb], w=[lgb], scale=-1.0)
        k.act(lg[:], lg[:], AF.Ln, r=[lgb], w=[lgb], bias=1.0)
        k.ts(lg[:], lg[:], -1.0, ALU.mult, r=[lgb], w=[lgb])
        k.memset(wq3[:, 0, :], 1.0, w=[wq3b], eng="dve")
        k.act(wq3[:, 1, :], lg[:, 0:8], AF.Exp, r=[lgb, eownb], w=[wq3b], scale=eown[:, 0:1])
        k.act(wq3[:, 2, :], lg[:, 8:16], AF.Exp, r=[lgb, eownb], w=[wq3b], scale=eown[:, 1:2])
        k.memset(wk3[:, 0, :], sc, w=[wk3b], eng="dve")
        k.act(wk3[:, 1, :], lg[:, 0:8], AF.Exp, r=[lgb, eownb], w=[wk3b], scale=eown[:, 2:3])
        k.act(wk3[:, 2, :], lg[:, 8:16], AF.Exp, r=[lgb, eownb], w=[wk3b], scale=eown[:, 3:4])
        k.ts(wk3[:, 1:3, :], wk3[:, 1:3, :], sc, ALU.mult, r=[wk3b], w=[wk3b])
        k.act(gC[:], lg[:], AF.Exp, r=[lgb], w=[gCb], scale=float(P))
        for h in range(8):
            k.act(mtmp[:, 0, :], dm[:, 0, :], AF.Exp, r=[dmb, lgb], w=[mtmpb], scale=lg[:, h:h + 1])
            k.act(mtmp[:, 1, :], dm[:, 2, :], AF.Exp, r=[dmb, lgb], w=[mtmpb], scale=lg[:, 8 + h:9 + h])
            k.tt(mtmp[:], mtmp[:], dm[:, 1:4:2, :], ALU.mult, r=[mtmpb, dmb], w=[mtmpb])
            k.tt(MT[:, h, :], mtmp[:, 0, :], mtmp[:, 1, :], ALU.add, r=[mtmpb], w=[MTb])
        for d_ in range(2):
            k.tt(wfo[:, d_], eo[:, d_, :].unsqueeze(2).to_broadcast([P, 24, 8]),
                 lg[:, 8 * d_:8 * d_ + 8].unsqueeze(1).to_broadcast([P, 24, 8]), ALU.mult,
                 r=[eob, lgb], w=[wfob])
        k.act(wfo[:], wfo[:], AF.Exp, r=[wfob], w=[wfob])
        k.ts(wfo[:], wfo[:], sc, ALU.mult, r=[wfob], w=[wfob])
        k.memset(S_in[:], 0.0, w=[S_inb], eng="dve")
        k.dump("lg", lg[:], lgb, [P, 16], F32)
        k.dump("MT", MT[:], MTb, [P, 8, P], F32)

        k.release(mDt)
        k.ri = 0

        def rope(pt, pb, tab, tabb):
            i = k.ri % 2
            k.ri += 1
            ro, rob = rout[i]
            X, Xb = rX[0]
            Y, Yb = rY[0]
            tv = pt[:].rearrange("p (h two i) -> p h two i", h=4, two=2)
            cc = tab[:, 0:1, :].unsqueeze(1).to_broadcast([P, 4, 2, 64])
            ss = tab[:, 1:2, :].unsqueeze(1).to_broadcast([P, 4, 2, 64])
            k.tt(X[:], tv, cc, ALU.mult, r=[pb, tabb], w=[Xb])
            k.tt(Y[:], tv, ss, ALU.mult, r=[pb, tabb], w=[Yb])
            k.tt(ro[:, :, 0, :], X[:, :, 0, :], Y[:, :, 1, :], ALU.subtract, r=[Xb, Yb], w=[rob])
            k.tt(ro[:, :, 1, :], Y[:, :, 0, :], X[:, :, 1, :], ALU.add, r=[Xb, Yb], w=[rob])
            return ro, rob

        mD1 = k.mark()
        st = make_stage()
        load_gain(st, g_mix)
        hTo, _hTob0 = k.sb("hTo", [P, KC, TOK], BF16, at=oreg)
        hTob = [Buf("hTo%d" % i, k.inh) for i in range(8)]
        kfo, kfob = k.sb("kfo", [P, 2, 8, 8, P], BF16, at=oreg + 32768)
        vo = [k.sb("vo%d" % i, [P, 8, 512], BF16) for i in range(1)]
        ro_t = [k.sb("ro_t%d" % i, [P, 8, 2, 64], F32) for i in range(1)]
        for bi in range(3):
            rt, rtb = ro_t[0]
            k.dma("sp", rt[:], rope_oth_in[:, bi * 8:(bi + 1) * 8], w=[rtb], key=("rot", 0))
            def rms_o1(ti, bi=bi):
                g = bi * 8 + ti
                rms1(st, g, src_dram=x_oth[g * P:(g + 1) * P, :])

            def rms_o2(ti, bi=bi):
                g = bi * 8 + ti
                rms2(st, g, hTo, hTob[ti], ti * P)

            for cb in range(2):
                wA, wB = wpair(w_in, [(C_RK + cb * 512, 512)])
                pend = {}

                def p1(ti, wA=wA, wB=wB):
                    pt, pb = ps()
                    gemm_tok(pt[:], pb, hTo, hTob[ti], ti * P, wA, wB, 0, 512)
                    pend[ti] = (pt, pb)

                def p2(ti, cb=cb, bi=bi, rt=rt, rtb=rtb):
                    pt, pb = pend.pop(ti)
                    ro, rob = rope(pt, pb, rt[:, ti], rtb)
                    rv = ro[:].rearrange("p h two i -> p h (two i)")
                    for d_ in range(2):
                        k.tt(kfo[:, d_, ti, cb * 4:(cb + 1) * 4, :], rv,
                             wfo[:, d_, bi * 8 + ti, cb * 4:(cb + 1) * 4].unsqueeze(2).to_broadcast([P, 4, P]),
                             ALU.mult, r=[rob, wfob], w=[kfob])

                if cb == 0:
                    rms_o1(0)
                    rms_o1(1)
                    rms_o2(0)
                    for ti in range(8):
                        p1(ti)
                        if ti + 2 < 8:
                            rms_o1(ti + 2)
                        if ti + 1 < 8:
                            rms_o2(ti + 1)
                        if ti >= 1:
                            p2(ti - 1)
                    p2(7)
                else:
                    p1(0)
                    for ti in range(8):
                        if ti + 1 < 8:
                            p1(ti + 1)
                        p2(ti)
            for cb in range(4):
                wA, wB = wpair(w_in, [(C_RV + cb * 512, 512)])
                vt, vb = vo[0]
                for ti in range(8):
                    pt, pb = ps()
                    gemm_tok(pt[:], pb, hTo, hTob[ti], ti * P, wA, wB, 0, 512)
                    k.copy(vt[:, ti, :], pt[:], r=[pb], w=[vb], eng="act")
                for d_ in range(2):
                    pk, pkb = ps()
                    k.mmg([(pk[:, hl * 256:(hl + 1) * 256],
                            [(kfo[:, d_, ti, 2 * cb + hl, :], vt[:, ti, hl * 256:(hl + 1) * 256]) for ti in range(8)])
                           for hl in range(2)], r=[kfob, vb], w=[pkb])
                    sv = S_in[:, d_, 2 * cb:2 * cb + 2, :].rearrange("p h e -> p (h e)")
                    k.tt(sv, pk[:], sv, ALU.add, r=[pkb, S_inb], w=[S_inb])
        k.dump("S_in", S_in[:], S_inb, [P, 2, 8, 256], F32)
        k.release(mD1)

        o_retT, o_retTb = k.sb("o_retT", [P, 16, TOK], BF16, at=oreg)
        qkT, qkTb = k.sb("qkT", [P, 8, 8, P], BF16, at=oreg + 32768)
        k3, k3b = k.sb("k3", [P, 8, 3, 2, P], BF16, at=oreg + 49152)
        q3s = [k.sb("q3s%d" % i, [P, 3, 2, P], BF16) for i in range(2)]
        vv, vvb = k.sb("vv", [P, 8, 512], BF16)
        gsg, gsgb = k.sb("gsg", [P, 8, 512], BF16)
        gnb, gnbb = k.sb("gnb", [P, 512], F32)
        Sbf, Sbfb = k.sb("Sbf", [P, 2, 8, 256], BF16)
        sff = [[k.sb("sff%d_%d" % (hl, i), [P, 256], F32) for i in range(2)] for hl in range(2)]
        sbf32 = [[k.sb("sbf%d_%d" % (hl, i), [P, 256], F32) for i in range(2)] for hl in range(2)]
        sbb = [k.sb("sbb%d" % i, [P, 2, 256], BF16) for i in range(2)]
        Pm = [k.sb("Pm%d" % i, [P, 2, P], BF16) for i in range(2)]
        gst = [k.sb("gst%d" % i, [P, 2, 6], F32) for i in range(2)]
        gmv = [k.sb("gmv%d" % i, [P, 2, 5], F32) for i in range(2)]
        yb_ = [k.sb("yb%d" % i, [P, 256], F32) for i in range(2)]
        ot = [k.sb("ot%d" % i, [P, 512], BF16) for i in range(2)]
        for hp in range(4):
            k.dma("sp", gnb[:], gn_g[:, hp * 512:(hp + 1) * 512].partition_broadcast(P), w=[gnbb])
            wqkA, wqkB = wpair(w_in, [(C_RQ + hp * 256, 256), (C_RK + hp * 256, 256)])
            wvA, wvB = wpair(w_in, [(C_RV + hp * 512, 512)])
            pend = {}

            def q1(n):
                pt, pb = ps()
                gemm_tok(pt[:], pb, hT, hTb, n * P, wqkA, wqkB, 0, 512)
                pend[n] = (pt, pb)

            def q2(n):
                pt, pb = pend.pop(n)
                ro, rob = rope(pt, pb, rope_own[:, n], rope_ownb)
                rv = ro[:].rearrange("p h two i -> p h (two i)")
                q3, q3b = q3s[n % 2]
                k.tt(q3[:], rv[:, 0:2, :].unsqueeze(1).to_broadcast([P, 3, 2, P]),
                     wq3[:, :, 2 * hp:2 * hp + 2].unsqueeze(3).to_broadcast([P, 3, 2, P]), ALU.mult,
                     r=[rob, wq3b], w=[q3b])
                k.tt(k3[:, n], rv[:, 2:4, :].unsqueeze(1).to_broadcast([P, 3, 2, P]),
                     wk3[:, :, 2 * hp:2 * hp + 2].unsqueeze(3).to_broadcast([P, 3, 2, P]), ALU.mult,
                     r=[rob, wk3b], w=[k3b])
                ptT, ptTb = ps()
                pv = ptT[:].bitcast(BF16)
                lst = []
                for v in range(3):
                    for hl in range(2):
                        lst.append((pv[:, (v * 2 + hl) * P:(v * 2 + hl + 1) * P], q3[:, v, hl, :]))
                for hl in range(2):
                    lst.append((pv[:, (6 + hl) * P:(7 + hl) * P], k3[:, n, 0, hl, :]))
                k.tr(lst, ident[:], r=[q3b, k3b, identb], w=[ptTb])
                k.copy(qkT[:, n], pv.rearrange("p (c t) -> p c t", c=8), r=[ptTb], w=[qkTb], eng="act")

            q1(0)
            for n in range(8):
                if n + 1 < 8:
                    q1(n + 1)
                q2(n)
            for n in range(8):
                pt, pb = ps()
                gemm_tok(pt[:], pb, hT, hTb, n * P, wvA, wvB, 0, 512)
                k.copy(vv[:, n, :], pt[:], r=[pb], w=[vvb], eng="act")
            wgA, wgB = wpair(w_in, [(C_RG + hp * 512, 512)])
            cur = []
            for hl in range(2):
                h = 2 * hp + hl
                k.copy(Sbf[:, hl, 0, :], S_in[:, 0, h, :], r=[S_inb], w=[Sbfb], eng="act")
                cur.append((S_in[:, 0, h, :], S_inb))
            for n in range(8):
                pt, pb = ps()
                gemm_tok(pt[:], pb, hT, hTb, n * P, wgA, wgB, 0, 512)
                k.act(gsg[:, n, :], pt[:], AF.Silu, r=[pb], w=[gsgb])
                k.tt(gsg[:, n, :], gsg[:, n, :], gnb[:], ALU.mult, r=[gsgb, gnbb], w=[gsgb])
                if n < 7:
                    pk, pkb = ps()
                    k.mmg([(pk[:, hl * 256:(hl + 1) * 256], [(k3[:, n, 1, hl, :], vv[:, n, hl * 256:(hl + 1) * 256])])
                           for hl in range(2)], r=[k3b, vvb], w=[pkb])
                    for hl in range(2):
                        h = 2 * hp + hl
                        nx, nxb = sff[hl][n % 2]
                        k.stt(nx[:], cur[hl][0], gC[:, h:h + 1], pk[:, hl * 256:(hl + 1) * 256], ALU.mult, ALU.add,
                              r=[cur[hl][1], gCb, pkb], w=[nxb])
                        k.copy(Sbf[:, hl, n + 1, :], nx[:], r=[nxb], w=[Sbfb], eng="act")
                        cur[hl] = (nx[:], nxb)
            curb = []
            for hl in range(2):
                h = 2 * hp + hl
                curb.append((S_in[:, 1, h, :], S_inb))
            live = {}

            def stA1(idx):
                n = 7 - idx
                sb_t, sb_b = sbb[idx % 2]
                for hl in range(2):
                    k.copy(sb_t[:, hl, :], curb[hl][0], r=[curb[hl][1]], w=[sb_b], eng="act")
                if n > 0:
                    pk, pkb = psb[4 + idx % 2]
                    k.mmg([(pk[:, hl * 256:(hl + 1) * 256], [(k3[:, n, 2, hl, :], vv[:, n, hl * 256:(hl + 1) * 256])])
                           for hl in range(2)], r=[k3b, vvb], w=[pkb])
                    live[("pk", idx)] = (pk, pkb)
                pst, pstb = psb[0 + idx % 2]
                k.mmg([(pst[:, hl * P:(hl + 1) * P], [(qkT[:, n, 6 + hl, :], qkT[:, n, hl, :])]) for hl in range(2)],
                      r=[qkTb], w=[pstb])
                pm, pmb = Pm[idx % 2]
                k.tt(pm[:], pst[:, 0:256].rearrange("p (h i) -> p h i", h=2), MT[:, 2 * hp:2 * hp + 2, :], ALU.mult,
                     r=[pstb, MTb], w=[pmb])
                po, pob = psb[2 + idx % 2]
                k.mmg([(po[:, hl * 256:(hl + 1) * 256],
                        [(pm[:, hl, :], vv[:, n, hl * 256:(hl + 1) * 256]),
                         (qkT[:, n, 2 + hl, :], Sbf[:, hl, n, :]),
                         (qkT[:, n, 4 + hl, :], sb_t[:, hl, :])]) for hl in range(2)],
                      r=[pmb, vvb, qkTb, Sbfb, sb_b], w=[pob])
                live[idx] = (po, pob)

            def stA2(idx):
                n = 7 - idx
                if n > 0:
                    pk, pkb = live.pop(("pk", idx))
                    for hl in range(2):
                        h = 2 * hp + hl
                        nx, nxb = sbf32[hl][idx % 2]
                        k.stt(nx[:], curb[hl][0], gC[:, 8 + h:9 + h], pk[:, hl * 256:(hl + 1) * 256], ALU.mult, ALU.add,
                              r=[curb[hl][1], gCb, pkb], w=[nxb])
                        curb[hl] = (nx[:], nxb)

            def stB1(idx):
                po, pob = live[idx]
                gs, gsb = gst[idx % 2]
                gm, gmb = gmv[idx % 2]
                for hl in range(2):
                    k.s.op("dve", lambda h_, o=gs[:, hl, :], i=po[:, hl * 256:(hl + 1) * 256]: h_.bn_stats(out=o, in_=i),
                           r=[pob], w=[gsb])
                for hl in range(2):
                    k.s.op("dve", lambda h_, o=gm[:, hl, 0:2], i=gs[:, hl, :]: h_.bn_aggr(out=o, in_=i),
                           r=[gsb], w=[gmb])
                k.act(gm[:, :, 2], gm[:, :, 1], AF.Sqrt, r=[gmb], w=[gmb], bias=EPS)

            def stB2(idx):
                n = 7 - idx
                po, pob = live.pop(idx)
                gm, gmb = gmv[idx % 2]
                k.recip(gm[:, :, 3], gm[:, :, 2], r=[gmb], w=[gmb])
                k.stt(gm[:, :, 4], gm[:, :, 0], -1.0, gm[:, :, 3], ALU.mult, ALU.mult, r=[gmb], w=[gmb])
                o_t, o_b = ot[idx % 2]
                for hl in range(2):
                    y_t, y_b = yb_[hl]
                    k.act(y_t[:], po[:, hl * 256:(hl + 1) * 256], AF.Identity, r=[pob, gmb], w=[y_b],
                          scale=gm[:, hl, 3:4], bias=gm[:, hl, 4:5])
                    k.tt(o_t[:, hl * 256:(hl + 1) * 256], y_t[:], gsg[:, n, hl * 256:(hl + 1) * 256], ALU.mult,
                         r=[y_b, gsgb], w=[o_b])
                ptT, ptTb = psb[6 + idx % 2]
                pv = ptT[:].bitcast(BF16)
                k.tr([(pv[:, c * P:(c + 1) * P], o_t[:, c * P:(c + 1) * P]) for c in range(4)], ident[:],
                     r=[o_b, identb], w=[ptTb])
                k.copy(o_retT[:, hp * 4:(hp + 1) * 4, n * P:(n + 1) * P],
                       pv[:, 0:512].rearrange("p (c t) -> p c t", c=4), r=[ptTb], w=[o_retTb], eng="act")

            stA1(0)
            stA2(0)
            for idx in range(8):
                stB1(idx)
                if idx + 1 < 8:
                    stA1(idx + 1)
                stB2(idx)
                if idx + 1 < 8:
                    stA2(idx + 1)
        k.dump("o_retT", o_retT[:], o_retTb, [P, 16, TOK], BF16)
        k.release(mD)


    def qknorm(pr_list, prbs, inv_n, dst_list, dstb, gain_cols, gainb, sqs, rbs):
        n = pr_list[0].shape[-1]
        sq_used = []
        for i, (pr, prb) in enumerate(zip(pr_list, prbs)):
            sqt, sqb = sqs[(k.sqi + i) % len(sqs)]
            k.act(sqt[:, 0:n], pr, AF.Square, r=[prb], w=[sqb])
            sq_used.append((sqt, sqb))
        k.sqi += len(pr_list)
        pss, pssb = ps()
        k.mm(pss[:, 0:n], [(ones[:], sqt[:, 0:n]) for sqt, _ in sq_used], r=[onesb] + [b for _, b in sq_used], w=[pssb])
        rbt, rbb = rbs[k.rbi % len(rbs)]
        k.rbi += 1
        k.act(rbt[:, 0:n], pss[:, 0:n], AF.Ln, r=[pssb], w=[rbb], scale=inv_n, bias=EPS)
        k.act(rbt[:, 0:n], rbt[:, 0:n], AF.Exp, r=[rbb], w=[rbb], scale=-0.5)
        for pr, prb, dst, gc in zip(pr_list, prbs, dst_list, gain_cols):
            k.stt(dst, pr, gc, rbt[:, 0:n], ALU.mult, ALU.mult, r=[prb, gainb, rbb], w=[dstb])

    k.sqi = 0
    k.rbi = 0

    if on("C"):
        o_naT, o_naTb = k.sb("o_naT", [P, 8, TOK], BF16, at=oreg + 32768)
        mC = k.mark()
        hTh, hThb = k.sb("hTh", [P, KC, TOK], BF16)
        mst = k.mark()
        st = make_stage(3)
        load_gain(st, g_mix)
        rmsnorm_seq(st, 8, hTh, lambda i: hThb, lambda i: i * P, lambda i: dict(src_dram=x_halo[i * P:(i + 1) * P, :]))
        k.release(mst)
        nag, nagb = k.sb("nag", [P, 2], F32)
        k.dma("sp", nag[:], nag_in, w=[nagb])
        k.ts(nag[:, 0:1], nag[:, 0:1], float(P) ** -0.5, ALU.mult, r=[nagb], w=[nagb])
        qT, qTb = k.sb("qT", [P, 2, TOK], BF16)
        kT, kTb = k.sb("kT", [P, 2, 2048], BF16)
        v_e, v_eb = k.sb("v_e", [P, 16, 256], BF16)
        v_o, v_ob = k.sb("v_o", [P, 15, 256], BF16)
        Est, Estb = k.sb("Est", [P, 1024], F32)
        Etab = [k.sb("E%d" % i, [P, 8, 512], BF16, at=oreg + 49152 + i * 8192) for i in range(2)]
        sqs = [k.sb("sq%d" % i, [P, 512], BF16) for i in range(2)]
        rbs = [k.sb("rb%d" % i, [P, 512], F32) for i in range(2)]
        eS = [k.sb("eS%d" % i, [P, 512], BF16) for i in range(3)]
        PmN = [k.sb("PmN%d" % i, [P, 512], BF16) for i in range(3)]
        rden, rdenb = k.sb("rden", [P, 512], F32)

        def win_src(blk):
            if blk == 0:
                return (lambda kc: hTh[:, kc, 0:512]), hThb
            if blk == 1:
                return (lambda kc: hT[:, kc, 0:512]), hTb
            if blk == 2:
                return (lambda kc: hT[:, kc, 512:1024]), hTb
            return (lambda kc: hTh[:, kc, 512:1024]), hThb

        for hp in range(4):
            wqkA, wqkB = wpair(w_in, [(C_NAQ + hp * 256, 256), (C_NAK + hp * 256, 256)])
            wvA, wvB = wpair(w_in, [(C_NAV + hp * 256, 256)])
            def epiece(i):
                hl, qd = i // 4, i % 4
                h = 2 * hp + hl
                et, etb = Etab[hl]
                k.dma("sp", Est[:], nab_in[:, h, qd * 1024:(qd + 1) * 1024], w=[Estb], key="Est")
                k.act(et[:, qd * 2:(qd + 1) * 2, :].rearrange("p c f -> p (c f)"), Est[:], AF.Exp, r=[Estb], w=[etb])

            jobs = []
            for hl in range(2):
                for tb in range(2):
                    jobs.append((((lambda kc, tb=tb: hT[:, kc, tb * 512:(tb + 1) * 512]), hTb), hl * P,
                                 qT[:, hl, tb * 512:(tb + 1) * 512], qTb, nag[:, 0:1]))
                for blk in range(4):
                    jobs.append((win_src(blk), 256 + hl * P, kT[:, hl, blk * 512:(blk + 1) * 512], kTb, nag[:, 1:2]))
            pend = {}

            def j1(i):
                src, c0, dst, dstb, gcol = jobs[i]
                pr, prb = ps()
                gemm_feat(pr[:], prb, src, wqkA, wqkB, c0)
                sqt, sqb = sqs[i % 2]
                k.act(sqt[:], pr[:], AF.Square, r=[prb], w=[sqb])
                pend[i] = (pr, prb, sqt, sqb)

            def j2(i):
                src, c0, dst, dstb, gcol = jobs[i]
                pr, prb, sqt, sqb = pend.pop(i)
                pss, pssb = ps()
                k.mm(pss[:], [(ones[:], sqt[:])], r=[onesb, sqb], w=[pssb])
                rbt, rbb = rbs[i % 2]
                k.act(rbt[:], pss[:], AF.Ln, r=[pssb], w=[rbb], scale=1.0 / P, bias=EPS)
                k.act(rbt[:], rbt[:], AF.Exp, r=[rbb], w=[rbb], scale=-0.5)
                k.stt(dst, pr[:], gcol, rbt[:], ALU.mult, ALU.mult, r=[prb, nagb, rbb], w=[dstb])

            j1(0)
            for i in range(len(jobs)):
                if i + 1 < len(jobs):
                    j1(i + 1)
                j2(i)
                if i < 8:
                    epiece(i)
            for m in range(16):
                if m < 4:
                    src, srcb, t0_ = hTh, hThb, m * P
                elif m < 12:
                    src, srcb, t0_ = hT, hTb, (m - 4) * P
                else:
                    src, srcb, t0_ = hTh, hThb, 512 + (m - 12) * P
                pv_, pvb = ps()
                gemm_tok(pv_[:, 0:256], pvb, src, srcb, t0_, wvA, wvB, 0, 256)
                k.copy(v_e[:, m, :], pv_[:, 0:256], r=[pvb], w=[v_eb], eng="act")
            k.dma("sp", v_o[0:64, :, :], v_e[64:128, 0:15, :], r=[v_eb], w=[v_ob])
            k.dma("sp", v_o[64:128, :, :], v_e[0:64, 1:16, :], r=[v_eb], w=[v_ob])
            items = [(hl, r) for hl in range(2) for r in range(16)]
            state = {}
            k.ps_ring = [0, 1, 2, 3]

            def s1(i):
                hl, r = items[i]
                t0, t1 = na_tiles(r)
                nw = (t1 - t0) * 64
                pS, pSb = ps()
                k.mmg([(pS[:, (t - t0) * 64:(t - t0 + 1) * 64],
                        [(kT[:, hl, (r + 2 * t) * 64:(r + 2 * t) * 64 + P], qT[:, hl, r * 64:(r + 1) * 64])])
                       for t in range(t0, t1)], r=[kTb, qTb], w=[pSb])
                es, esb = eS[i % 3]
                k.act(es[:, 0:nw], pS[:, 0:nw], AF.Exp, r=[pSb], w=[esb])
                pm, pmb = PmN[i % 3]
                et, etb = Etab[hl]
                k.tt(pm[:, 0:nw], es[:, 0:nw], et[:, na_cls(r), t0 * 64:t1 * 64], ALU.mult, r=[esb, etb], w=[pmb])
                state[i] = (pm, pmb)

            def s2(i):
                hl, r = items[i]
                h = 2 * hp + hl
                pm, pmb = state.pop(i)
                r8 = r % 8
                if r8 == 0:
                    pair = state.get("pair", 0)
                    state["pair"] = pair + 1
                    state["O"] = psb[4 + 2 * (pair % 2)]
                    state["D"] = psb[5 + 2 * (pair % 2)]
                pO, pOb = state["O"]
                pD, pDb = state["D"]
                vt, vtb = (v_e, v_eb) if r % 2 == 0 else (v_o, v_ob)
                m0_ = r // 2
                t0, t1 = na_tiles(r)
                k.mmg([(pO[:, r8 * 64:(r8 + 1) * 64],
                        [(vt[:, m0_ + t, hl * P:(hl + 1) * P], pm[:, (t - t0) * 64:(t - t0 + 1) * 64]) for t in range(t0, t1)]),
                       (pD[:, r8 * 64:(r8 + 1) * 64],
                        [(ones[:], pm[:, (t - t0) * 64:(t - t0 + 1) * 64]) for t in range(t0, t1)])],
                      r=[vtb, pmb, onesb], w=[pOb, pDb])
                if r8 == 7:
                    k.act(rden[:], pD[:], AF.Ln, r=[pDb], w=[rdenb])
                    k.act(rden[:], rden[:], AF.Exp, r=[rdenb], w=[rdenb], scale=-1.0)
                    k.tt(o_naT[:, h, (r - 7) * 64:(r + 1) * 64], pO[:], rden[:], ALU.mult, r=[pOb, rdenb], w=[o_naTb])

            s1(0)
            s1(1)
            for i in range(len(items)):
                if i + 2 < len(items):
                    s1(i + 2)
                s2(i)
            k.ps_ring = list(range(8))
        k.dump("o_naT", o_naT[:], o_naTb, [P, 8, TOK], BF16)
        k.release(mC)

    if on("B"):
        o_memT, o_memTb = k.sb("o_memT", [P, 8, TOK], BF16, at=oreg + 49152)
        mB = k.mark()
        memT, memTb = k.sb("memT", [P, KC, 256], BF16)
        mst = k.mark()
        st = make_stage()
        load_gain(st, g_mem)
        for i in range(2):
            rmsnorm_tile(st, i, memT, memTb, i * P, src_dram=mem_in[i * P:(i + 1) * P, :])
        k.release(mst)
        xag, xagb = k.sb("xag", [P, 4], F32)
        k.dma("sp", xag[:], xag_in, w=[xagb])
        k.ts(xag[:, 0:2], xag[:, 0:2], 256.0 ** -0.5, ALU.mult, r=[xagb], w=[xagb])
        kxT, kxTb = k.sb("kxT", [P, 4, 2, 256], BF16)
        vx, vxb = k.sb("vx", [P, 2, 1024], BF16)
        sqs = [k.sb("sqx%d" % i, [P, 512], BF16) for i in range(4)]
        rbs = [k.sb("rbx%d" % i, [P, 512], F32) for i in range(2)]
        qx = [k.sb("qx%d" % i, [P, 2, 512], BF16) for i in range(2)]
        PT = [k.sb("PT%d" % i, [P, 2, 512], BF16) for i in range(2)]
        rdx = [k.sb("rdx%d" % i, [P, 512], F32) for i in range(2)]
        for hx2 in range(2):
            wA, wB = wpair(w_mkv, [(hx2 * 512, 512)])
            for hl in range(2):
                hx = 2 * hx2 + hl
                pr, prb = ps()
                for c in range(2):
                    gemm_feat(pr[:, c * 256:(c + 1) * 256], prb, ((lambda kc: memT[:, kc, :]), memTb), wA, wB, (hl * 2 + c) * P)
                qknorm([pr[:, 0:256], pr[:, 256:512]], [prb, prb], 1.0 / 256, [kxT[:, hx, 0, :], kxT[:, hx, 1, :]], kxTb,
                       [xag[:, 2:3], xag[:, 3:4]], xagb, sqs, rbs)
        for cb in range(2):
            wA, wB = wpair(w_mkv, [(1024 + cb * 512, 512)])
            for mt in range(2):
                pv_, pvb = ps()
                gemm_tok(pv_[:], pvb, memT, memTb, mt * P, wA, wB, 0, 512)
                k.copy(vx[:, mt, cb * 512:(cb + 1) * 512], pv_[:], r=[pvb], w=[vxb], eng="act")
        iters = [(hx2, hl, tb) for hx2 in range(2) for hl in range(2) for tb in range(2)]
        wts = {}
        pendx = {}

        def x1(i):
            hx2, hl, tb = iters[i]
            if hx2 not in wts:
                wts[hx2] = wpair(w_in, [(C_XQ + hx2 * 512, 512)])
            wA, wB = wts[hx2]
            prs = [psb[0 + 2 * (i % 2)], psb[1 + 2 * (i % 2)]]
            sq_used = []
            for c in range(2):
                gemm_feat(prs[c][0][:], prs[c][1], ((lambda kc, tb=tb: hT[:, kc, tb * 512:(tb + 1) * 512]), hTb),
                          wA, wB, (hl * 2 + c) * P)
            for c in range(2):
                sqt, sqb = sqs[(2 * i + c) % 4]
                k.act(sqt[:], prs[c][0][:], AF.Square, r=[prs[c][1]], w=[sqb])
                sq_used.append((sqt, sqb))
            pendx[i] = (prs, sq_used)

        def x2(i):
            hx2, hl, tb = iters[i]
            hx = 2 * hx2 + hl
            prs, sq_used = pendx.pop(i)
            pss, pssb = ps()
            k.mm(pss[:], [(ones[:], sqt[:]) for sqt, _ in sq_used], r=[onesb] + [b for _, b in sq_used], w=[pssb])
            rbt, rbb = rbs[i % 2]
            k.act(rbt[:], pss[:], AF.Ln, r=[pssb], w=[rbb], scale=1.0 / 256, bias=EPS)
            k.act(rbt[:], rbt[:], AF.Exp, r=[rbb], w=[rbb], scale=-0.5)
            qt, qb = qx[i % 2]
            for c in range(2):
                k.stt(qt[:, c, :], prs[c][0][:], xag[:, c:c + 1], rbt[:], ALU.mult, ALU.mult,
                      r=[prs[c][1], xagb, rbb], w=[qb])
            ptt, ptb = PT[i % 2]
            for mt in range(2):
                pS, pSb = ps()
                k.mm(pS[:], [(kxT[:, hx, c, mt * P:(mt + 1) * P], qt[:, c, :]) for c in range(2)], r=[kxTb, qb], w=[pSb])
                k.act(ptt[:, mt, :], pS[:], AF.Exp, r=[pSb], w=[ptb])
            pOs = [ps(), ps()]
            for c in range(2):
                k.mm(pOs[c][0][:], [(vx[:, mt, hx * 256 + c * P:hx * 256 + (c + 1) * P], ptt[:, mt, :]) for mt in range(2)],
                     r=[vxb, ptb], w=[pOs[c][1]])
            pD, pDb = ps()
            k.mm(pD[:], [(ones[:], ptt[:, mt, :]) for mt in range(2)], r=[onesb, ptb], w=[pDb])
            rd, rdb = rdx[i % 2]
            k.act(rd[:], pD[:], AF.Ln, r=[pDb], w=[rdb])
            k.act(rd[:], rd[:], AF.Exp, r=[rdb], w=[rdb], scale=-1.0)
            for c in range(2):
                k.tt(o_memT[:, hx * 2 + c, tb * 512:(tb + 1) * 512], pOs[c][0][:], rd[:], ALU.mult,
                     r=[pOs[c][1], rdb], w=[o_memTb])

        k.ps_ring = [4, 5, 6, 7]
        x1(0)
        for i in range(len(iters)):
            if i + 1 < len(iters):
                x1(i + 1)
            x2(i)
        k.ps_ring = list(range(8))
        k.dump("o_memT", o_memT[:], o_memTb, [P, 8, TOK], BF16)
        k.release(mB)

    if on("E"):
        mergedT, mergedTb = k.sb("mergedT", [P, KC, TOK], BF16)
        mE = k.mark()
        macc, maccb = k.sb("macc", [P, 4, TOK], F32)
        sig8 = [k.sb("sig8_%d" % i, [P, 8, 512], BF16) for i in range(2)]
        tmpE = [k.sb("tmpE%d" % i, [P, 512], F32) for i in range(2)]
        branches = [(C_GNA, w_bna, o_naT, o_naTb, 8), (C_GRET, w_bret, o_retT, o_retTb, 16), (C_GMEM, w_bmem, o_memT, o_memTb, 8)]
        it = 0
        gi = 0
        for cb in range(4):
            for bi_, (cg, wbr, oT, oTb, nk) in enumerate(branches):
                gA, gB = wpair(w_in, [(cg + cb * 512, 512)])
                sg8, sg8b = sig8[gi % 2]
                gi += 1
                for ct in range(4):
                    for tb in range(2):
                        pg, pgb = ps()
                        gemm_feat(pg[:], pgb, ((lambda kc, tb=tb: hT[:, kc, tb * 512:(tb + 1) * 512]), hTb), gA, gB, ct * P)
                        k.act(sg8[:, ct * 2 + tb, :], pg[:], AF.Sigmoid, r=[pgb], w=[sg8b])
                bA = wload(wbr, 0, [(cb * 512, 512)])
                bB = wload(wbr, 1024, [(cb * 512, 512)]) if nk == 16 else None
                for ct in range(4):
                    for tb in range(2):
                        pbr, pbrb = ps()
                        k.mm(pbr[:], [(bA[0][:, kc, ct * P:(ct + 1) * P], oT[:, kc, tb * 512:(tb + 1) * 512]) for kc in range(8)],
                             r=[oTb, bA[1]], w=[pbrb], start=True, stop=(bB is None))
                        if bB is not None:
                            k.mm(pbr[:], [(bB[0][:, kc, ct * P:(ct + 1) * P], oT[:, 8 + kc, tb * 512:(tb + 1) * 512]) for kc in range(8)],
                                 r=[oTb, bB[1]], w=[pbrb], start=False, stop=True)
                        sgs = sg8[:, ct * 2 + tb, :]
                        mslice = macc[:, ct, tb * 512:(tb + 1) * 512]
                        if bi_ == 0:
                            k.tt(mslice, pbr[:], sgs, ALU.mult, r=[pbrb, sg8b], w=[maccb])
                        else:
                            tm, tmb = tmpE[it % 2]
                            k.tt(tm[:], pbr[:], sgs, ALU.mult, r=[pbrb, sg8b], w=[tmb])
                            if bi_ == 1:
                                k.tt(mslice, mslice, tm[:], ALU.add, r=[maccb, tmb], w=[maccb])
                            else:
                                k.tt(mergedT[:, cb * 4 + ct, tb * 512:(tb + 1) * 512], mslice, tm[:], ALU.add,
                                     r=[maccb, tmb], w=[mergedTb])
                        it += 1
        k.dump("mergedT", mergedT[:], mergedTb, [P, KC, TOK], BF16)
        k.release(mE)

    if on("F"):
        k.release(mE)
        x1, x1b = k.sb("x1", [P, NT, D], F32, at=oreg)
        for ti in range(NT):
            k.dma("sp", x1[:, ti, :], x_own[ti * P:(ti + 1) * P, :], w=[x1b], key="x1ld")
        for cb in range(4):
            wA, wB = wpair(w_out, [(cb * 512, 512)])
            for ti in range(NT):
                pt, pb = ps()
                gemm_tok(pt[:], pb, mergedT, mergedTb, ti * P, wA, wB, 0, 512)
                xs_ = x1[:, ti, cb * 512:(cb + 1) * 512]
                k.tt(xs_, pt[:], xs_, ALU.add, r=[pb, x1b], w=[x1b])
        k.dump("x1", x1[:], x1b, [P, NT, D], F32)
        st = make_stage()
        load_gain(st, g_ffn)
        rmsnorm_seq(st, NT, hT, lambda i: hTb, lambda i: i * P, lambda i: dict(src_sb=(x1[:, i, :], x1b)))

    if on("G"):
        k.release(mE)
        uT = [k.sb("uT%d" % i, [P, 8, TOK], BF16) for i in range(2)]
        sqf = [k.sb("sqf%d" % i, [P, 512], F32) for i in range(2)]
        it = 0
        for fb in range(8):
            ut, utb = uT[fb % 2]
            for cbk in range(2):
                wA, wB = wpair(w_ff1, [(fb * 1024 + cbk * 512, 512)])
                for ct in range(4):
                    for tb in range(2):
                        pu, pub = ps()
                        gemm_feat(pu[:], pub, ((lambda kc, tb=tb: hT[:, kc, tb * 512:(tb + 1) * 512]), hTb), wA, wB, ct * P)
                        sq_, sqb_ = sqf[it % 2]
                        k.act(sq_[:], pu[:], AF.Square, r=[pub], w=[sqb_])
                        k.stt(ut[:, cbk * 4 + ct, tb * 512:(tb + 1) * 512], pu[:], 0.0, sq_[:], ALU.is_gt, ALU.mult,
                              r=[pub, sqb_], w=[utb])
                        it += 1
            for cb in range(4):
                wU = wload(w_ff2, fb * 1024, [(cb * 512, 512)])
                for ti in range(NT):
                    pt, pb = ps()
                    k.mm(pt[:], [(ut[:, kc, ti * P:(ti + 1) * P], wU[0][:, kc, :]) for kc in range(8)], r=[utb, wU[1]], w=[pb])
                    xs_ = x1[:, ti, cb * 512:(cb + 1) * 512]
                    k.tt(xs_, pt[:], xs_, ALU.add, r=[pb, x1b], w=[x1b])
        for ti in range(NT):
            oid = k.dma("sp", y_out[ti * P:(ti + 1) * P, :], x1[:, ti, :], r=[x1b], key="yst")
            s.final_waits.append(oid)

    with nc.Block() as block:
        s.emit(block)
    return k


def _rope_tab(pos):
    inv = np.power(np.float32(10000.0), -np.arange(64, dtype=np.float32) / np.float32(64)).astype(np.float32)
    ang = (pos.astype(np.float32)[:, None] * inv[None, :]).astype(np.float32)
    return np.stack([np.cos(ang), np.sin(ang)], axis=1).astype(np.float32)


def _consts(j):
    c = {}
    c["ident_in"] = np.eye(P, dtype=np.float32)
    own0 = j * TOK
    pos = own0 + np.arange(TOK)
    c["rope_own"] = np.ascontiguousarray(_rope_tab(pos).reshape(8, P, 2, 64).transpose(1, 0, 2, 3))
    pos_o, ef, eb = [], [], []
    for s_ in range(3):
        q = (j + 1 + s_) % 4
        pg = q * TOK + np.arange(TOK)
        pos_o.append(pg)
        ef.append(np.where(q < j, own0 - 1 - pg, BIG))
        eb.append(np.where(q > j, pg - (own0 + TOK), BIG))
    pos_o = np.concatenate(pos_o)
    c["rope_oth"] = np.ascontiguousarray(_rope_tab(pos_o).reshape(24, P, 2, 64).transpose(1, 0, 2, 3))
    ef = np.concatenate(ef).reshape(24, P).T
    eb = np.concatenate(eb).reshape(24, P).T
    c["e_oth"] = np.ascontiguousarray(np.stack([ef, eb], axis=1).astype(np.float32))
    a = np.arange(P, dtype=np.float32)
    c["e_own"] = np.stack([a + 1, P - a, P - 1 - a, a], axis=1).astype(np.float32)
    jj = np.arange(P)[:, None]
    ii = np.arange(P)[None, :]
    c["dmask"] = np.stack([np.maximum(ii - jj, 0), (jj <= ii), np.maximum(jj - ii, 0), (jj > ii)],
                          axis=1).astype(np.float32)
    return c


def host_inputs(inp, core):
    b, j = core // 4, core % 4
    m = dict(_consts(j))
    x = inp["x"]
    m["x_own"] = np.ascontiguousarray(x[b, j * TOK:(j + 1) * TOK])
    m["x_oth"] = np.ascontiguousarray(np.concatenate(
        [x[b, ((j + 1 + s_) % 4) * TOK:((j + 1 + s_) % 4 + 1) * TOK] for s_ in range(3)], axis=0))
    halo = np.zeros((TOK, D), np.float32)
    t0 = (16 * j - 8) * 64
    if t0 >= 0:
        halo[0:512] = x[b, t0:t0 + 512]
    t1 = (16 * j + 16) * 64
    if t1 + 512 <= 4096:
        halo[512:1024] = x[b, t1:t1 + 512]
    m["x_halo"] = halo
    m["mem_b"] = np.ascontiguousarray(inp["mem"][b])
    m["g_mix"] = np.ascontiguousarray(inp["norm_mix_g"][0][None, :])
    m["g_ffn"] = np.ascontiguousarray(inp["norm_ffn_g"][0][None, :])
    m["g_mem"] = np.ascontiguousarray(inp["mem_norm_g"][0][None, :])
    m["gn_g"] = np.ascontiguousarray(inp["ret_gn_g"][0][None, :])
    m["dlog"] = np.concatenate([inp["ret_decay_logit_fwd"][0], inp["ret_decay_logit_bwd"][0]])[None, :].astype(np.float32)
    m["nag"] = np.ascontiguousarray(np.stack([inp["na_q_norm_g"][0], inp["na_k_norm_g"][0]], axis=1))
    m["xag"] = np.ascontiguousarray(np.concatenate(
        [inp["xa_q_norm_g"][0].reshape(2, P).T, inp["xa_k_norm_g"][0].reshape(2, P).T], axis=1))
    m["na_bias"] = _na_bias(inp["na_rpb"][0], j)
    m["w_in"] = inp["w_in"][0]
    m["w_mem_kv"] = inp["w_mem_kv"][0]
    m["w_br_na"] = inp["w_br_na"][0]
    m["w_br_ret"] = inp["w_br_ret"][0]
    m["w_br_mem"] = inp["w_br_mem"][0]
    m["w_out"] = inp["w_out"][0]
    m["w_ff1"] = inp["w_ff1"][0]
    m["w_ff2"] = inp["w_ff2"][0]
    return m


NA_CLS_ROWS = [None, 0, 1, 2, 3, 13, 14, 15]


def na_cls(r):
    if 4 <= r <= 12:
        return 0
    return 1 + r if r < 4 else 5 + (r - 13)


def na_tiles(r):
    if r <= 1:
        return 2, 8
    if r <= 3:
        return 2, 7
    if r <= 12:
        return 2, 6
    if r <= 14:
        return 1, 6
    return 0, 6


def _na_bias(rpb, j):
    half = np.arange(2)[:, None, None, None, None]
    kc = np.arange(64)[None, :, None, None, None]
    cl = np.arange(8)[None, None, :, None, None]
    t = np.arange(8)[None, None, None, :, None]
    qc = np.arange(64)[None, None, None, None, :]
    rloc = np.array([6, 0, 1, 2, 3, 13, 14, 15])[cl]
    g = 16 * j + rloc
    rel = -8 + 2 * t + half
    start = np.clip(g - 4, 0, 56)
    krow = g + rel
    vrow = (krow >= start) & (krow <= start + 7)
    cstart = np.clip(qc - 8, 0, 48)
    vcol = (kc >= cstart) & (kc < cstart + 16)
    valid = np.broadcast_to(vrow & vcol, (2, 64, 8, 8, 64))
    dr = np.broadcast_to(np.clip(rel + 7, 0, 14), (2, 64, 8, 8, 64))
    dc = np.broadcast_to(np.clip(kc - qc, -15, 15) + 15, (2, 64, 8, 8, 64))
    out = np.empty((2, 64, 8, 8, 8, 64), np.float32)
    for h in range(8):
        out[:, :, h] = np.where(valid, rpb[h][dr, dc], np.float32(NEG))
    return np.ascontiguousarray(out.reshape(P, 8, 4096))


_CACHE = {}


def kernel(**inputs):
    inp = {k_: np.asarray(v) for k_, v in inputs.items()}
    if "k" not in _CACHE:
        _CACHE["k"] = build()
    k = _CACHE["k"]
    in_maps = []
    for c in range(8):
        m = host_inputs(inp, c)
        in_maps.append({n: np.ascontiguousarray(m[n], dtype=np.float32) for n in k.din})
    res = run_bass_kernel_spmd(k.nc, in_maps, core_ids=list(range(8)))
    out = np.empty((2, 4096, D), np.float32)
    for c in range(8):
        b, j = c // 4, c % 4
        out[b, j * TOK:(j + 1) * TOK] = res.results[c]["y_own"]
    return out
```
